# Optimizing a Trainium2 kernel written in Bass

```python
import math
import jax, jax.numpy as jnp
from jax import lax
import numpy as np


D_MODEL = 1024
BATCH = 8
SEQ = 8192
DEPTH = 1
DEC_BATCH = 2
DEC_SEQ = 16384
PAST_LEN = 128

A_HEADS = 4
A_QK_DIM = 64
A_V_DIM = 2 * A_QK_DIM
A_WIDTH = A_HEADS * A_V_DIM
A_ROPE_DIM = A_QK_DIM // 4
B_HEADS = 4
B_NOPE_DIM = 128
B_ROPE_DIM = 64
B_V_DIM = 128
B_WIDTH = B_HEADS * B_V_DIM
Q_LORA = 384
KV_LORA = 256
MIX_WIDTH = A_WIDTH + B_WIDTH
ROPE_THETA = 500000.0
NORM_EPS = 1e-6
Q_BLOCK = 128

IN_SPLIT = (2 * A_HEADS * A_QK_DIM,
            2 * A_HEADS * A_QK_DIM,
            A_WIDTH,
            A_WIDTH,
            Q_LORA,
            KV_LORA,
            B_ROPE_DIM,
            B_WIDTH)
IN_COLS = 2 * A_HEADS * A_QK_DIM * 2 + 2 * A_WIDTH + Q_LORA + KV_LORA + B_ROPE_DIM + B_WIDTH

kernel_name = 'hymba_diff_mla_encoder'


def rmsnorm(x, g):
    xf = x.astype(jnp.float32)
    y = xf * lax.rsqrt(jnp.mean(xf * xf, axis=-1, keepdims=True) + NORM_EPS)
    return (y * g.astype(jnp.float32)).astype(x.dtype)


def split_cols(x, sizes):
    idx = []
    acc = 0
    for n in sizes[:-1]:
        acc += n
        idx.append(acc)
    return jnp.split(x, idx, axis=-1)


def rope_tables(seq, dim):
    inv_freq = jnp.float32(ROPE_THETA) ** (-jnp.arange(0, dim, 2, dtype=jnp.float32) / dim)
    ang = jnp.arange(seq, dtype=jnp.float32)[:, None] * inv_freq[None, :]
    return jnp.cos(ang), jnp.sin(ang)


def apply_rope(x, cos, sin):
    half = x.shape[-1] // 2
    x1, x2 = x[..., :half], x[..., half:]
    cos = cos.astype(x.dtype)
    sin = sin.astype(x.dtype)
    return jnp.concatenate([x1 * cos - x2 * sin, x2 * cos + x1 * sin], axis=-1)


def sweep_query_blocks(block_fn, queries):
    b, s = queries[0].shape[:2]
    nb = s // Q_BLOCK
    blocks = tuple(jnp.moveaxis(q.reshape((b, nb, Q_BLOCK) + q.shape[2:]), 1, 0) for q in queries)
    out = lax.map(lambda qb: block_fn(*qb), blocks)
    out = jnp.moveaxis(out, 0, 1)
    return out.reshape((b, s) + out.shape[3:])


def diff_attention(q1, q2, k1, k2, v, lam):
    scale = A_QK_DIM ** -0.5

    def block(q1b, q2b):
        s1 = jnp.einsum('bqhd,bkhd->bhqk', q1b, k1).astype(jnp.float32) * scale
        s2 = jnp.einsum('bqhd,bkhd->bhqk', q2b, k2).astype(jnp.float32) * scale
        w = jax.nn.softmax(s1, axis=-1) - lam * jax.nn.softmax(s2, axis=-1)
        return jnp.einsum('bhqk,bkhe->bqhe', w.astype(v.dtype), v)

    return sweep_query_blocks(block, (q1, q2))


def mla_attention(qn, qr, kn, kr, v):
    scale = (B_NOPE_DIM + B_ROPE_DIM) ** -0.5

    def block(qnb, qrb):
        s = (jnp.einsum('bqhd,bkhd->bhqk', qnb, kn)
             + jnp.einsum('bqhr,bkr->bhqk', qrb, kr)).astype(jnp.float32) * scale
        p = jax.nn.softmax(s, axis=-1)
        return jnp.einsum('bhqk,bkhe->bqhe', p.astype(v.dtype), v)

    return sweep_query_blocks(block, (qn, qr))


def encoder_layer(x, c, layer, w_ada, b_ada, g_pre, w_in, lambda_q1, lambda_k1, lambda_q2, lambda_k2,
                  g_subln, g_cq, w_uq, g_ckv, w_ukv, w_out, g_post):
    b, s, _ = x.shape
    lambda_init = 0.8 - 0.6 * math.exp(-0.3 * layer)

    mod = jax.nn.silu(c) @ w_ada + b_ada
    shift, scale, gate = jnp.split(mod, 3, axis=-1)
    h = rmsnorm(x, g_pre) * (1 + scale[:, None, :]) + shift[:, None, :]

    proj = h @ w_in
    aq, ak, av, ag, cq, ckv, kr, bg = split_cols(proj, IN_SPLIT)

    cos_a, sin_a = rope_tables(s, A_ROPE_DIM)
    cos_a4, sin_a4 = cos_a[:, None, None, :], sin_a[:, None, None, :]
    aq = aq.reshape(b, s, A_HEADS, 2, A_QK_DIM)
    ak = ak.reshape(b, s, A_HEADS, 2, A_QK_DIM)
    aq = jnp.concatenate([apply_rope(aq[..., :A_ROPE_DIM], cos_a4, sin_a4), aq[..., A_ROPE_DIM:]], axis=-1)
    ak = jnp.concatenate([apply_rope(ak[..., :A_ROPE_DIM], cos_a4, sin_a4), ak[..., A_ROPE_DIM:]], axis=-1)
    av = av.reshape(b, s, A_HEADS, A_V_DIM)
    lam = (jnp.exp(jnp.sum(lambda_q1.astype(jnp.float32) * lambda_k1.astype(jnp.float32)))
           - jnp.exp(jnp.sum(lambda_q2.astype(jnp.float32) * lambda_k2.astype(jnp.float32)))
           + lambda_init)
    oa = diff_attention(aq[..., 0, :], aq[..., 1, :], ak[..., 0, :], ak[..., 1, :], av, lam)
    oa = rmsnorm(oa, g_subln) * (1.0 - lambda_init)
    ya = oa.reshape(b, s, A_WIDTH) * jax.nn.silu(ag)

    cos_b, sin_b = rope_tables(s, B_ROPE_DIM)
    q = (rmsnorm(cq, g_cq) @ w_uq).reshape(b, s, B_HEADS, B_NOPE_DIM + B_ROPE_DIM)
    qn, qr = q[..., :B_NOPE_DIM], q[..., B_NOPE_DIM:]
    qr = apply_rope(qr, cos_b[:, None, :], sin_b[:, None, :])
    kv = (rmsnorm(ckv, g_ckv) @ w_ukv).reshape(b, s, B_HEADS, B_NOPE_DIM + B_V_DIM)
    kn, vb = kv[..., :B_NOPE_DIM], kv[..., B_NOPE_DIM:]
    kr = apply_rope(kr, cos_b, sin_b)
    ob = mla_attention(qn, qr, kn, kr, vb)
    yb = ob.reshape(b, s, B_WIDTH) * jax.nn.silu(bg)

    out = jnp.concatenate([ya, yb], axis=-1) @ w_out
    return x + gate[:, None, :] * rmsnorm(out, g_post)


def setup_inputs(seed: int = 0) -> dict:
    key = jax.random.key(seed)
    ks = jax.random.split(key, 24)
    f32 = jnp.float32
    nrm = lambda k, shape, s: jax.random.normal(k, shape, f32) * s
    gain = lambda k, shape: 1.0 + 0.02 * jax.random.normal(k, shape, f32)
    L = DEPTH
    return {
        'x_prompt': nrm(ks[0], (BATCH, SEQ, D_MODEL), 1.0),
        'x_sample': nrm(ks[1], (DEC_BATCH, DEC_SEQ, D_MODEL), 1.0),
        'c_prompt': nrm(ks[2], (BATCH, D_MODEL), 1.0),
        'c_sample': nrm(ks[3], (DEC_BATCH, D_MODEL), 1.0),
        'w_ada': nrm(ks[4], (L, D_MODEL, 3 * D_MODEL), 0.3 * D_MODEL ** -0.5),
        'b_ada': nrm(ks[5], (L, 3 * D_MODEL), 0.01),
        'g_pre': gain(ks[6], (L, D_MODEL)),
        'w_in': nrm(ks[7], (L, D_MODEL, IN_COLS), D_MODEL ** -0.5),
        'lambda_q1': nrm(ks[8], (L, A_QK_DIM), 0.1),
        'lambda_k1': nrm(ks[9], (L, A_QK_DIM), 0.1),
        'lambda_q2': nrm(ks[10], (L, A_QK_DIM), 0.1),
        'lambda_k2': nrm(ks[11], (L, A_QK_DIM), 0.1),
        'g_subln': gain(ks[12], (L, A_V_DIM)),
        'g_cq': gain(ks[13], (L, Q_LORA)),
        'w_uq': nrm(ks[14], (L, Q_LORA, B_HEADS * (B_NOPE_DIM + B_ROPE_DIM)), Q_LORA ** -0.5),
        'g_ckv': gain(ks[15], (L, KV_LORA)),
        'w_ukv': nrm(ks[16], (L, KV_LORA, B_HEADS * (B_NOPE_DIM + B_V_DIM)), KV_LORA ** -0.5),
        'w_out': nrm(ks[17], (L, MIX_WIDTH, D_MODEL), MIX_WIDTH ** -0.5),
        'g_post': gain(ks[18], (L, D_MODEL)),
    }


def reference(x_prompt, x_sample, c_prompt, c_sample, w_ada, b_ada, g_pre, w_in, lambda_q1, lambda_k1,
              lambda_q2, lambda_k2, g_subln, g_cq, w_uq, g_ckv, w_ukv, w_out, g_post):
    y_prompt = x_prompt
    y_sample = x_sample
    for layer in range(DEPTH):
        lp = (w_ada[layer], b_ada[layer], g_pre[layer], w_in[layer], lambda_q1[layer], lambda_k1[layer],
              lambda_q2[layer], lambda_k2[layer], g_subln[layer], g_cq[layer], w_uq[layer], g_ckv[layer],
              w_ukv[layer], w_out[layer], g_post[layer])
        y_prompt = encoder_layer(y_prompt, c_prompt, layer, *lp)
        y_sample = encoder_layer(y_sample, c_sample, layer, *lp)
    return (y_prompt, y_sample)
```

```python
import os
import numpy as np
import ml_dtypes
import concourse.bass as bass
import concourse.mybir as mybir
from concourse.ap import AP
from concourse.bass_utils import run_bass_kernel_spmd
from contextlib import ExitStack

F32 = mybir.dt.float32
BF16 = mybir.dt.bfloat16
AF = mybir.ActivationFunctionType
ALU = mybir.AluOpType
AX = mybir.AxisListType

D = 1024
C_AQ, C_AK, C_AV, C_AG, C_CQ, C_CKV, C_KR, C_BG = 0, 512, 1024, 1536, 2048, 2432, 2688, 2752
INC = 3264
EPS = 1e-6
NSUB = 2
NST = 3
ENGS = ('pe', 'act', 'dve', 'pool', 'sp')
LAMBDA_INIT = 0.8 - 0.6 * 1.0
SBUF_WORDS = 207 * 256


class Sem:
    __slots__ = ('h', 'n', 'name')

    def __init__(self, h, name):
        self.h = h
        self.n = 0
        self.name = name


class Op:
    __slots__ = ('eng', 'fn', 'deps', 'sem', 'inc', 'flag', 'cnt', 'isdma')


class Res:
    __slots__ = ('w', 'r')

    def __init__(self):
        self.w = None
        self.r = {}


class Kern:
    def __init__(self, nc, es):
        self.nc = nc
        self.es = es
        self.ops = {e: [] for e in ENGS}
        self.esem = {e: self.newsem('e_' + e) for e in ENGS}
        self.bar_deps = []
        self.bar_pending = set()
        self.dma_out = []
        self.nsem = 0

    def newsem(self, name):
        h = self.es.enter_context(self.nc.semaphore(name))
        return Sem(h, name)

    def op(self, eng, fn, r=(), w=(), dsem=None):
        o = Op()
        o.eng = eng
        o.fn = fn
        o.isdma = dsem is not None
        o.sem = dsem if dsem is not None else self.esem[eng]
        o.inc = 16 if o.isdma else 1
        o.flag = o.isdma
        o.cnt = None
        deps = {}
        for x in r:
            if x.w is not None:
                deps[id(x.w)] = x.w
        for x in w:
            xw = x.w
            if xw is not None and (o.isdma or xw.isdma or xw.eng != eng or eng != 'pe'):
                deps[id(xw)] = xw
            for rd in x.r.values():
                if o.isdma or rd.isdma or rd.eng != eng or eng != 'pe':
                    deps[id(rd)] = rd
        if eng in self.bar_pending:
            self.bar_pending.discard(eng)
            for d in self.bar_deps:
                deps[id(d)] = d
        o.deps = list(deps.values())
        for x in r:
            x.r[('d', id(o)) if o.isdma else eng] = o
        for x in w:
            x.w = o
            x.r = {}
        self.ops[eng].append(o)
        if o.isdma:
            self.dma_out.append(o)
        return o

    def barrier(self):
        deps = []
        for e in ENGS:
            for o in reversed(self.ops[e]):
                if not o.isdma:
                    deps.append(o)
                    break
        deps.extend(self.dma_out)
        self.dma_out = []
        self.bar_deps = deps
        self.bar_pending = set(ENGS)

    def finalize(self, block):
        for e in ENGS:
            for o in self.ops[e]:
                for d in o.deps:
                    d.flag = True
        final = list(self.dma_out)
        for e in ENGS:
            for o in self.ops[e]:
                if o.flag:
                    o.sem.n += o.inc
                    o.cnt = o.sem.n
        maxc = max(s.n for s in self.esem.values())
        assert maxc < 60000, maxc

        def replay(eng_name, e):
            waited = {}
            for o in self.ops[eng_name]:
                for d in o.deps:
                    s = d.sem
                    if waited.get(id(s), 0) < d.cnt:
                        e.wait_ge(s.h, d.cnt)
                        waited[id(s)] = d.cnt
                ins = o.fn(e)
                if o.flag:
                    ins.then_inc(o.sem.h, o.inc)
            if eng_name == 'sp':
                for o in final:
                    if waited.get(id(o.sem), 0) < o.cnt:
                        e.wait_ge(o.sem.h, o.cnt)
                        waited[id(o.sem)] = o.cnt

        @block.tensor
        def _(t):
            replay('pe', t)

        @block.scalar
        def _(a):
            replay('act', a)

        @block.vector
        def _(v):
            replay('dve', v)

        @block.gpsimd
        def _(g):
            replay('pool', g)

        @block.sync
        def _(s):
            replay('sp', s)


class Arena:
    def __init__(self, t):
        self.t = t
        self.off = 0

    def alloc(self, shape, dt, parts=128):
        free = 1
        for s in shape:
            free *= s
        nb = free * (2 if dt == BF16 else 4)
        words = (nb + 3) // 4
        words = (words + 7) // 8 * 8
        st = self.off
        self.off += words
        assert self.off <= SBUF_WORDS, (self.off, SBUF_WORDS)
        v = self.t[0:parts, st:st + words]
        if dt == BF16:
            v = v.bitcast(BF16)
        v = v[:, 0:free]
        return reshape(v, shape)


def reshape(v, shape):
    if len(shape) == 1:
        return v
    names = 'abcde'[:len(shape)]
    kw = {names[i]: shape[i] for i in range(1, len(shape))}
    return v.rearrange('p (' + ' '.join(names) + ') -> p ' + ' '.join(names), **kw)


def mk(base, dims):
    return AP(base.tensor, base.offset, [list(base.ap[0])] + [[int(s), int(c)] for s, c in dims])


def build(SPr, SSm, NQS):
    nc = bass.Bass("TRN2", target_bir_lowering=False)

    def din(name, shape, dt=F32):
        return nc.dram_tensor(name, list(shape), dt, kind="ExternalInput").ap()

    def dout(name, shape, dt=F32):
        return nc.dram_tensor(name, list(shape), dt, kind="ExternalOutput").ap()

    def dscr(name, shape, dt=BF16):
        return nc.dram_tensor(name, list(shape), dt, kind="Internal").ap()

    jobs = []
    for jn, S, NQ in (('P', SPr, SPr), ('S', SSm, NQS)):
        jb = dict(S=S, NQ=NQ, name=jn)
        jb['xkv'] = din('xkv' + jn, [S, D])
        jb['tab'] = din('tab' + jn, [S, 160])
        if jn == 'P':
            jb['xq'] = jb['xkv']
            jb['tabq'] = jb['tab']
        else:
            jb['xq'] = din('xq' + jn, [NQ, D])
            jb['tabq'] = din('tabq' + jn, [NQ, 160])
        jb['y'] = dout('y' + jn, [NQ, D])
        jb['KT'] = dscr('KT' + jn, [8, 128, S])
        jb['KrT'] = dscr('KrT' + jn, [64, S])
        jb['V'] = dscr('V' + jn, [8, 128, S // 128, 129])
        jb['QT'] = dscr('QT' + jn, [8, 128, NQ])
        jb['QrT'] = dscr('QrT' + jn, [4, 64, NQ])
        jb['G'] = dscr('G' + jn, [8, 128, NQ // 128, 128])
        jb['Y'] = dscr('Y' + jn, [NQ, D])
        jb['GGP'] = dscr('GGP' + jn, [128, D], F32)
        jobs.append(jb)
    ccol_d = din('ccol', [128, 16])
    wada_d = din('w_ada', [D, 3 * D])
    bada_d = din('bada_bc', [128, 3 * D])
    gpre_d = din('gpre_bc', [128, D])
    gpost_d = din('gpost_bc', [128, D])
    gsub_d = din('gsub_bc', [128, 128])
    lam_d = din('lam_bc', [128, 256])
    gcol_d = din('gcol', [128, 8])
    win_d = din('w_in', [D, INC])
    wuq_d = din('w_uq', [384, 768])
    wukv_d = din('w_ukv', [256, 1024])
    wout_d = din('w_out', [D, D])
    ident_d = din('ident', [128, 128], BF16)

    with ExitStack() as es:
        sb_t = es.enter_context(nc.sbuf_tensor("arena", [128, SBUF_WORDS], F32))
        ps_t = es.enter_context(nc.psum_tensor("psarena", [128, 4096], F32))
        K = Kern(nc, es)
        PS = ps_t[:, :]

        def psf(bank, n=512, off=0):
            return PS[:, bank * 512 + off: bank * 512 + off + n]

        def psb(bank, n=1024, off=0, parts=128):
            v = PS[0:parts, bank * 512: (bank + 1) * 512].bitcast(BF16)
            return v[:, off:off + n]

        A = Arena(sb_t[:, :])
        ident = A.alloc([128], BF16)
        w_in_sb = A.alloc([8, INC], BF16)
        w_uq_sb = A.alloc([3, 768], BF16)
        w_ukv_sb = A.alloc([2, 1024], BF16)
        gsb = [A.alloc([D], F32) for _ in range(2)]
        shb = [A.alloc([D], F32) for _ in range(2)]
        gcol = A.alloc([8], F32)
        Rident, Rwin, Rwuq, Rwukv, Rgcol = Res(), Res(), Res(), Res(), Res()
        Rgs = [Res(), Res()]
        Rsh = [Res(), Res()]
        mark_proj = A.off

        K.op('sp', lambda e: e.dma_start(out=ident, in_=ident_d), w=[Rident], dsem=K.newsem('ld_id'))
        K.op('sp', lambda e: e.dma_start(out=gcol, in_=gcol_d), w=[Rgcol], dsem=K.newsem('ld_gc'))

        bada = A.alloc([3 * D], F32)
        gpre = A.alloc([D], F32)
        gpost = A.alloc([D], F32)
        ccol = A.alloc([16], F32)
        scv = A.alloc([16], F32)
        screp = A.alloc([8, 128], F32)
        wst = [A.alloc([INC], F32) for _ in range(2)]
        tmpm = A.alloc([D], F32)
        tmpg = A.alloc([D], F32)
        Rbada, Rgpre, Rgpost, Rccol, Rscv, Rscrep, Rtmpm, Rtmpg = (Res() for _ in range(8))
        Rwst = [Res(), Res()]
        wst_sem = [K.newsem('wst0'), K.newsem('wst1')]
        Rpm = [Res() for _ in range(8)]
        K.op('sp', lambda e: e.dma_start(out=bada, in_=bada_d), w=[Rbada], dsem=K.newsem('ld_b'))
        K.op('sp', lambda e: e.dma_start(out=gpre, in_=gpre_d), w=[Rgpre], dsem=K.newsem('ld_gp'))
        K.op('sp', lambda e: e.dma_start(out=gpost, in_=gpost_d), w=[Rgpost], dsem=K.newsem('ld_gq'))
        K.op('sp', lambda e: e.dma_start(out=ccol, in_=ccol_d), w=[Rccol], dsem=K.newsem('ld_cc'))
        K.op('act', lambda e: e.activation(out=scv, in_=ccol, func=AF.Silu), r=[Rccol], w=[Rscv])
        wcnt = [0]
        ggp_sem = K.newsem('st_ggp')

        def stage_load(src, ncols):
            s = wcnt[0] % 2
            wcnt[0] += 1
            K.op('sp', lambda e: e.dma_start(out=wst[s][:, 0:ncols], in_=src), w=[Rwst[s]], dsem=wst_sem[s])
            return s

        for j in range(2):
            K.op('dve', lambda e, j=j: e.tensor_copy(out=screp, in_=mk(scv[:, j * 8:j * 8 + 8], [(1, 8), (0, 128)])),
                 r=[Rscv], w=[Rscrep])
            for fc in range(8):
                s = stage_load(wada_d[fc * 128:(fc + 1) * 128, :], 3 * D)
                for cc in range(6):
                    K.op('pe', lambda e, s=s, fc=fc, cc=cc: e.matmul(
                        psf(cc), lhsT=screp[:, fc, :], rhs=wst[s][:, cc * 512:(cc + 1) * 512],
                        start=(fc == 0), stop=(fc == 7)), r=[Rscrep, Rwst[s]], w=[Rpm[cc]])
            for hf in range(2):
                sl = slice(hf * 512, (hf + 1) * 512)
                K.op('dve', lambda e, hf=hf, sl=sl, j=j: e.tensor_tensor(
                    out=shb[j][:, sl], in0=psf(hf), in1=bada[:, sl], op=ALU.add),
                    r=[Rpm[hf], Rbada], w=[Rsh[j]])
                K.op('dve', lambda e, hf=hf, sl=sl: e.tensor_tensor(
                    out=tmpm[:, sl], in0=psf(2 + hf), in1=bada[:, D + hf * 512:D + (hf + 1) * 512], op=ALU.add),
                    r=[Rpm[2 + hf], Rbada], w=[Rtmpm])
                K.op('dve', lambda e, sl=sl, j=j: e.scalar_tensor_tensor(
                    out=gsb[j][:, sl], in0=tmpm[:, sl], scalar=1.0, in1=gpre[:, sl], op0=ALU.add, op1=ALU.mult),
                    r=[Rtmpm, Rgpre], w=[Rgs[j]])
                K.op('dve', lambda e, hf=hf, sl=sl: e.tensor_tensor(
                    out=tmpm[:, sl], in0=psf(4 + hf), in1=bada[:, 2 * D + hf * 512:2 * D + (hf + 1) * 512],
                    op=ALU.add), r=[Rpm[4 + hf], Rbada], w=[Rtmpm])
                K.op('dve', lambda e, sl=sl: e.tensor_tensor(
                    out=tmpg[:, sl], in0=tmpm[:, sl], in1=gpost[:, sl], op=ALU.mult),
                    r=[Rtmpm, Rgpost], w=[Rtmpg])
            K.op('sp', lambda e, j=j: e.dma_start(out=jobs[j]['GGP'], in_=tmpg), r=[Rtmpg], dsem=ggp_sem)
        for k in range(8):
            s = stage_load(win_d[k * 128:(k + 1) * 128, :], INC)
            K.op('dve' if k % 2 == 0 else 'pool', lambda e, s=s, k=k: e.tensor_copy(out=w_in_sb[:, k, :], in_=wst[s]),
                 r=[Rwst[s]], w=[Rwin])
        for k in range(3):
            s = stage_load(wuq_d[k * 128:(k + 1) * 128, :], 768)
            K.op('dve', lambda e, s=s, k=k: e.tensor_scalar(
                out=w_uq_sb[:, k, :], in0=wst[s][:, 0:768], scalar1=gcol[:, k:k + 1], scalar2=None, op0=ALU.mult),
                r=[Rwst[s], Rgcol], w=[Rwuq])
        for k in range(2):
            s = stage_load(wukv_d[k * 128:(k + 1) * 128, :], 1024)
            K.op('dve', lambda e, s=s, k=k: e.tensor_scalar(
                out=w_ukv_sb[:, k, :], in0=wst[s][:, 0:1024], scalar1=gcol[:, 3 + k:4 + k], scalar2=None,
                op0=ALU.mult), r=[Rwst[s], Rgcol], w=[Rwukv])
        K.barrier()
        KSTOP = int(os.environ.get('KSTOP', '9'))

        A.off = mark_proj
        NX = 4
        xs = [A.alloc([D], F32) for _ in range(NX)]
        NTB = 6
        tabs = [A.alloc([160], F32) for _ in range(NTB)]
        junk = A.alloc([D], BF16)
        stt = [A.alloc([8], F32) for _ in range(4)]
        tmp = [A.alloc([D], F32) for _ in range(2)]
        hb = [A.alloc([D], BF16) for _ in range(2)]
        hT = [A.alloc([8, 128], BF16) for _ in range(2)]
        ktok = [A.alloc([512], BF16) for _ in range(2)]
        ta = [A.alloc([8, 16], F32) for _ in range(2)]
        tb = [A.alloc([8, 16], F32) for _ in range(2)]
        ro = [A.alloc([8, 16], F32) for _ in range(2)]
        Rro = [Res(), Res()]
        c2b = [A.alloc([384], BF16) for _ in range(2)]
        c2T = [A.alloc([384], BF16) for _ in range(2)]
        krtok = [A.alloc([64], BF16) for _ in range(2)]
        tq_a = [A.alloc([4, 64], F32) for _ in range(2)]
        tq_b = [A.alloc([4, 64], F32) for _ in range(2)]
        n2tok = [A.alloc([4, 128], BF16) for _ in range(2)]
        qrtok = [A.alloc([4, 64], BF16) for _ in range(2)]
        KTst = [A.alloc([4, NSUB * 128], BF16) for _ in range(NST)]
        N2Tst = [A.alloc([4, NSUB * 128], BF16) for _ in range(NST)]
        RTst = [A.alloc([4, NSUB * 128], BF16) for _ in range(NST)]
        Vst = [A.alloc([NSUB, 8, 129], BF16) for _ in range(NST)]
        x_sem = [K.newsem('x%d' % i) for i in range(NX)]
        t_sem = [K.newsem('t%d' % i) for i in range(NTB)]
        st_sem = [[K.newsem('st%d_%d' % (a, b)) for b in range(NST)] for a in range(3 + NSUB)]
        Rx = [Res() for _ in range(NX)]
        Rtab = [Res() for _ in range(NTB)]
        Rst = [[Res() for _ in range(6)] for _ in range(4)]
        Rjunk = Res()
        Rtmp = [Res(), Res()]
        Rhb = [Res(), Res()]
        RhT = [Res(), Res()]
        Rktok = [Res(), Res()]
        Rta = [Res(), Res()]
        Rtb = [Res(), Res()]
        Rc2b = [Res(), Res()]
        Rc2T = [Res(), Res()]
        Rkrtok = [Res(), Res()]
        Rtqa = [Res(), Res()]
        Rtqb = [Res(), Res()]
        Rn2tok = [Res(), Res()]
        Rqrtok = [Res(), Res()]
        RKTst = [[Res() for _ in range(NSUB)] for _ in range(NST)]
        RN2Tst = [[Res() for _ in range(NSUB)] for _ in range(NST)]
        RRTst = [[Res() for _ in range(NSUB)] for _ in range(NST)]
        RVstA = [[Res() for _ in range(NSUB)] for _ in range(NST)]
        RVstB = [[Res() for _ in range(NSUB)] for _ in range(NST)]
        RpsT, RpsK, RpsC, RpKV = (Res() for _ in range(4))
        RpP = [Res(), Res(), Res()]
        psT = psb(0)
        pP = [psf(1), psf(2), psf(3)]
        psK = psb(4)
        psC = psb(5)
        psC64 = psb(5, parts=64)
        pKV = PS[:, 6 * 512: 8 * 512]
        bank_ctr = [0]
        sup_ctr = [0]
        for _s in range(NST):
            K.op('pool', lambda e, _s=_s: e.memset(Vst[_s].rearrange('p a b c -> p (a b) c')[:, :, 128:129], 1.0), w=RVstA[_s])

        def proj_pass(j, kind):
            jb = jobs[j]
            kv = (kind == 'kv')
            xsrc, tabsrc = (jb['xkv'], jb['tab']) if kv else (jb['xq'], jb['tabq'])
            NT = (jb['S'] if kv else jb['NQ']) // 128
            NT = min(NT, int(os.environ.get('KNT', '100000')))
            KP = int(os.environ.get('KP', '9'))
            assert NT % NSUB == 0
            if kv:
                chunks = [('rope', C_AK, 512), ('v', C_AV, 512), ('c2', C_CKV, 320)]
                nc2, nk, nout, w2, Rw2 = 256, 2, 1024, w_ukv_sb, Rwukv
            else:
                chunks = [('rope', C_AQ, 512), ('c2', C_CQ, 384), ('ga', C_AG, 512), ('gb', C_BG, 512)]
                nc2, nk, nout, w2, Rw2 = 384, 3, 768, w_uq_sb, Rwuq
            sup0 = sup_ctr[0]

            def slots(i):
                sup = sup0 + i // NSUB
                return i % NX, i % 2, sup % NST, i % NSUB, i % 4

            def ld(i):
                s, s2, ss, sub, s4 = slots(i)
                K.op('sp', lambda e: e.dma_start(out=xs[s], in_=xsrc[i * 128:(i + 1) * 128, :]), w=[Rx[s]],
                     dsem=x_sem[s])
                tbs = i % NTB
                K.op('sp', lambda e: e.dma_start(out=tabs[tbs], in_=tabsrc[i * 128:(i + 1) * 128, :]), w=[Rtab[tbs]],
                     dsem=t_sem[tbs])

            def pre(i):
                s, s2, ss, sub, s4 = slots(i)
                st = stt[s4]
                K.op('act', lambda e: e.activation(out=junk, in_=xs[s], func=AF.Square, accum_out=st[:, 0:1]),
                     r=[Rx[s]], w=[Rst[s4][0], Rjunk])
                K.op('act', lambda e: e.activation(out=st[:, 1:2], in_=st[:, 0:1], func=AF.Sqrt, scale=1.0 / D,
                                                   bias=EPS), r=[Rst[s4][0]], w=[Rst[s4][1]])
                K.op('dve', lambda e: e.reciprocal(out=st[:, 2:3], in_=st[:, 1:2]), r=[Rst[s4][1]], w=[Rst[s4][2]])
                K.op('dve', lambda e: e.scalar_tensor_tensor(out=tmp[s2], in0=xs[s], scalar=st[:, 2:3], in1=gsb[j],
                                                             op0=ALU.mult, op1=ALU.mult),
                     r=[Rx[s], Rst[s4][2], Rgs[j]], w=[Rtmp[s2]])
                K.op('pool', lambda e: e.tensor_tensor(out=hb[s2], in0=tmp[s2], in1=shb[j], op=ALU.add),
                     r=[Rtmp[s2], Rsh[j]], w=[Rhb[s2]])

            def T1(i):
                s, s2, ss, sub, s4 = slots(i)
                for c in range(8):
                    K.op('pe', lambda e, c=c: e.transpose(psT[:, c * 128:(c + 1) * 128],
                                                          hb[s2][:, c * 128:(c + 1) * 128], ident),
                         r=[Rhb[s2], Rident], w=[RpsT])
                K.op('act', lambda e: e.activation(out=hT[s2].rearrange('p a b -> p (a b)'), in_=psT, func=AF.Copy),
                     r=[RpsT], w=[RhT[s2]])

            def M(i):
                s, s2, ss, sub, s4 = slots(i)
                st = stt[s4]
                tab = tabs[i % NTB]
                for (ck, col0, n) in chunks:
                    bk = bank_ctr[0] % 3
                    bank_ctr[0] += 1
                    for c in range(8):
                        K.op('pe', lambda e, c=c, bk=bk, col0=col0, n=n: e.matmul(
                            pP[bk][:, 0:n], lhsT=hT[s2][:, c, :], rhs=w_in_sb[:, c, col0:col0 + n],
                            start=(c == 0), stop=(c == 7)), r=[RhT[s2], Rwin], w=[RpP[bk]])
                    pp = pP[bk]
                    KM = int(os.environ.get('KM', '255'))
                    if ck == 'rope' and not (KM & 2):
                        continue
                    if ck == 'v' and not (KM & 4):
                        continue
                    if ck == 'c2' and not (KM & 8):
                        continue
                    KR = int(os.environ.get('KR', '255'))
                    if ck == 'rope':
                      xr = ktok[s2].rearrange('p (g e) -> p g e', e=64)[:, :, 0:16]
                      cbc = mk(tab[:, 0:1], [(0, 8), (1, 16)])
                      sbc = mk(tab[:, 16:17], [(0, 8), (1, 16)])
                      kv3 = ktok[s2].rearrange('p (g e) -> p g e', e=64)
                      if KR & 1:
                        K.op('act', lambda e, pp=pp: e.activation(out=ktok[s2], in_=pp[:, 0:512], func=AF.Copy),
                             r=[RpP[bk]], w=[Rktok[s2]])
                      if KR & 2:
                        K.op('dve', lambda e, xr=xr, cbc=cbc: e.tensor_tensor(out=ta[s2], in0=xr, in1=cbc,
                                                                               op=ALU.mult),
                             r=[Rktok[s2], Rtab[i % NTB]], w=[Rta[s2]])
                        K.op('dve', lambda e, xr=xr, sbc=sbc: e.tensor_tensor(out=tb[s2], in0=xr, in1=sbc,
                                                                               op=ALU.mult),
                             r=[Rktok[s2], Rtab[i % NTB]], w=[Rtb[s2]])
                      if KR & 4:
                        E1 = os.environ.get('E1')
                        K.op('pool', lambda e: e.tensor_tensor(out=ro[s2], in0=ta[s2], in1=tb[s2], op=ALU.subtract) if E1 else e.tensor_tensor(out=ro[s2][:, :, 0:8], in0=ta[s2][:, :, 0:8],
                                                              in1=tb[s2][:, :, 8:16], op=ALU.subtract),
                             r=[Rta[s2], Rtb[s2]], w=[Rro[s2]])
                        K.op('pool', lambda e: e.tensor_tensor(out=ro[s2], in0=ta[s2], in1=tb[s2], op=ALU.add) if E1 else e.tensor_tensor(out=ro[s2][:, :, 8:16], in0=ta[s2][:, :, 8:16],
                                                              in1=tb[s2][:, :, 0:8], op=ALU.add),
                             r=[Rta[s2], Rtb[s2]], w=[Rro[s2]])
                        K.op('act', lambda e, kv3=kv3: e.activation(out=kv3[:, :, 0:16], in_=ro[s2], func=AF.Copy),
                             r=[Rro[s2]], w=[Rktok[s2]])
                    elif ck == 'v':
                        K.op('dve', lambda e, pp=pp: e.tensor_copy(
                            out=Vst[ss][:, sub, 0:4, 0:128], in_=pp[:, 0:512].rearrange('p (h e) -> p h e', e=128)),
                            r=[RpP[bk]], w=[RVstA[ss][sub]])
                    elif ck in ('ga', 'gb'):
                        h0 = 0 if ck == 'ga' else 4
                        K.op('act', lambda e, pp=pp, h0=h0: e.activation(
                            out=Vst[ss][:, sub, h0:h0 + 4, 0:128], in_=pp[:, 0:512].rearrange('p (h e) -> p h e', e=128),
                            func=AF.Silu), r=[RpP[bk]], w=[(RVstA if ck == 'ga' else RVstB)[ss][sub]])
                    else:
                        K.op('act', lambda e, pp=pp: e.activation(out=junk[:, 0:nc2], in_=pp[:, 0:nc2],
                                                                  func=AF.Square, accum_out=st[:, 3:4]),
                             r=[RpP[bk]], w=[Rst[s4][3], Rjunk])
                        K.op('act', lambda e: e.activation(out=st[:, 4:5], in_=st[:, 3:4], func=AF.Sqrt,
                                                           scale=1.0 / nc2, bias=EPS),
                             r=[Rst[s4][3]], w=[Rst[s4][4]])
                        K.op('dve', lambda e: e.reciprocal(out=st[:, 5:6], in_=st[:, 4:5]),
                             r=[Rst[s4][4]], w=[Rst[s4][5]])
                        ncp = 320 if kv else nc2
                        K.op('act', lambda e, pp=pp: e.activation(out=c2b[s2][:, 0:ncp], in_=pp[:, 0:ncp], func=AF.Copy),
                             r=[RpP[bk]], w=[Rc2b[s2]])
                        if kv:
                            xk = c2b[s2][:, 256:320]
                            cb = tab[:, 32:96]
                            sn = tab[:, 96:160]
                            qa = tq_a[s2][:, 0, :]
                            qb = tq_b[s2][:, 0, :]
                            K.op('dve', lambda e, xk=xk, cb=cb, qa=qa: e.tensor_tensor(out=qa, in0=xk, in1=cb,
                                                                                       op=ALU.mult),
                                 r=[Rc2b[s2], Rtab[i % NTB]], w=[Rtqa[s2]])
                            K.op('dve', lambda e, xk=xk, sn=sn, qb=qb: e.tensor_tensor(out=qb, in0=xk, in1=sn,
                                                                                       op=ALU.mult),
                                 r=[Rc2b[s2], Rtab[i % NTB]], w=[Rtqb[s2]])
                            K.op('dve', lambda e, qa=qa, qb=qb: e.tensor_tensor(
                                out=krtok[s2][:, 0:32], in0=qa[:, 0:32], in1=qb[:, 32:64], op=ALU.subtract),
                                r=[Rtqa[s2], Rtqb[s2]], w=[Rkrtok[s2]])
                            K.op('dve', lambda e, qa=qa, qb=qb: e.tensor_tensor(
                                out=krtok[s2][:, 32:64], in0=qa[:, 32:64], in1=qb[:, 0:32], op=ALU.add),
                                r=[Rtqa[s2], Rtqb[s2]], w=[Rkrtok[s2]])

            def Y(i):
                s, s2, ss, sub, s4 = slots(i)
                for h in range(4):
                    K.op('pe', lambda e, h=h: e.transpose(psK[:, h * 128:(h + 1) * 128],
                                                          ktok[s2][:, h * 128:(h + 1) * 128], ident),
                         r=[Rktok[s2], Rident], w=[RpsK])
                K.op('dve', lambda e: e.tensor_copy(
                    out=KTst[ss][:, :, sub * 128:(sub + 1) * 128],
                    in_=psK[:, 0:512].rearrange('p (h t) -> p h t', t=128)), r=[RpsK], w=[RKTst[ss][sub]])
                for k in range(nc2 // 128):
                    K.op('pe', lambda e, k=k: e.transpose(psC[:, k * 128:(k + 1) * 128],
                                                          c2b[s2][:, k * 128:(k + 1) * 128], ident),
                         r=[Rc2b[s2], Rident], w=[RpsC])
                K.op('act', lambda e: e.activation(out=c2T[s2][:, 0:nc2], in_=psC[:, 0:nc2], func=AF.Copy),
                     r=[RpsC], w=[Rc2T[s2]])
                if kv:
                    K.op('pe', lambda e: e.transpose(psC64[:, 384:512], krtok[s2][:, 0:64], ident),
                         r=[Rkrtok[s2], Rident], w=[RpsC])
                    K.op('act', lambda e: e.activation(out=RTst[ss][0:64, 0, sub * 128:(sub + 1) * 128],
                                                       in_=psC64[:, 384:512], func=AF.Copy), r=[RpsC], w=[RRTst[ss][sub]])

            def Z(i):
                s, s2, ss, sub, s4 = slots(i)
                st = stt[s4]
                for (o0, on) in ((0, 512), (512, nout - 512)):
                    for k in range(nk):
                        K.op('pe', lambda e, o0=o0, on=on, k=k: e.matmul(
                            pKV[:, o0:o0 + on], lhsT=c2T[s2][:, k * 128:(k + 1) * 128], rhs=w2[:, k, o0:o0 + on],
                            start=(k == 0), stop=(k == nk - 1)), r=[Rc2T[s2], Rw2], w=[RpKV])
                if kv:
                    v4 = pKV.rearrange('p (h e) -> p h e', e=256)
                    K.op('dve', lambda e: e.tensor_scalar(out=n2tok[s2], in0=v4[:, :, 0:128], scalar1=st[:, 5:6],
                                                          scalar2=None, op0=ALU.mult),
                         r=[RpKV, Rst[s4][5]], w=[Rn2tok[s2]])
                    K.op('dve', lambda e: e.tensor_scalar(out=Vst[ss][:, sub, 4:8, 0:128], in0=v4[:, :, 128:256],
                                                          scalar1=st[:, 5:6], scalar2=None, op0=ALU.mult),
                         r=[RpKV, Rst[s4][5]], w=[RVstB[ss][sub]])
                else:
                    v4 = pKV[:, 0:768].rearrange('p (h e) -> p h e', e=192)
                    K.op('dve', lambda e: e.tensor_scalar(out=n2tok[s2], in0=v4[:, :, 0:128], scalar1=st[:, 5:6],
                                                          scalar2=None, op0=ALU.mult),
                         r=[RpKV, Rst[s4][5]], w=[Rn2tok[s2]])
                    xq = mk(pKV[:, 128:129], [(192, 4), (1, 64)])
                    tab = tabs[i % NTB]
                    cb = mk(tab[:, 32:33], [(0, 4), (1, 64)])
                    sn = mk(tab[:, 96:97], [(0, 4), (1, 64)])
                    K.op('dve', lambda e: e.scalar_tensor_tensor(out=tq_a[s2], in0=xq, scalar=st[:, 5:6], in1=cb,
                                                                 op0=ALU.mult, op1=ALU.mult),
                         r=[RpKV, Rst[s4][5], Rtab[i % NTB]], w=[Rtqa[s2]])
                    K.op('dve', lambda e: e.scalar_tensor_tensor(out=tq_b[s2], in0=xq, scalar=st[:, 5:6], in1=sn,
                                                                 op0=ALU.mult, op1=ALU.mult),
                         r=[RpKV, Rst[s4][5], Rtab[i % NTB]], w=[Rtqb[s2]])
                    K.op('dve', lambda e: e.tensor_tensor(out=qrtok[s2][:, :, 0:32], in0=tq_a[s2][:, :, 0:32],
                                                          in1=tq_b[s2][:, :, 32:64], op=ALU.subtract),
                         r=[Rtqa[s2], Rtqb[s2]], w=[Rqrtok[s2]])
                    K.op('dve', lambda e: e.tensor_tensor(out=qrtok[s2][:, :, 32:64], in0=tq_a[s2][:, :, 32:64],
                                                          in1=tq_b[s2][:, :, 0:32], op=ALU.add),
                         r=[Rtqa[s2], Rtqb[s2]], w=[Rqrtok[s2]])

            def W(i):
                s, s2, ss, sub, s4 = slots(i)
                for h in range(4):
                    K.op('pe', lambda e, h=h: e.transpose(psK[:, 512 + h * 128:512 + (h + 1) * 128],
                                                          n2tok[s2][:, h, :], ident),
                         r=[Rn2tok[s2], Rident], w=[RpsK])
                K.op('dve', lambda e: e.tensor_copy(
                    out=N2Tst[ss][:, :, sub * 128:(sub + 1) * 128],
                    in_=psK[:, 512:1024].rearrange('p (h t) -> p h t', t=128)),
                    r=[RpsK], w=[RN2Tst[ss][sub]])
                if not kv:
                    for h in range(4):
                        K.op('pe', lambda e, h=h: e.transpose(psC64[:, 512 + h * 128:512 + (h + 1) * 128],
                                                              qrtok[s2][:, h, :], ident),
                             r=[Rqrtok[s2], Rident], w=[RpsC])
                    K.op('act', lambda e: e.activation(
                        out=RTst[ss][0:64, :, sub * 128:(sub + 1) * 128],
                        in_=psC64[:, 512:1024].rearrange('p (h t) -> p h t', t=128), func=AF.Copy),
                        r=[RpsC], w=[RRTst[ss][sub]])
                if sub == NSUB - 1:
                    t0 = (i - (NSUB - 1)) * 128
                    j0 = i - (NSUB - 1)
                    nt = NSUB * 128
                    if kv:
                        d_kt, d_n2 = jb['KT'][0:4, :, t0:t0 + nt], jb['KT'][4:8, :, t0:t0 + nt]
                        d_r = jb['KrT'][:, t0:t0 + nt]
                        s_r = RTst[ss][0:64, 0, :]
                        d_v = jb['V'][:, :, j0:j0 + NSUB, :]
                        s_v = Vst[ss]
                    else:
                        d_kt, d_n2 = jb['QT'][0:4, :, t0:t0 + nt], jb['QT'][4:8, :, t0:t0 + nt]
                        d_r = jb['QrT'][:, :, t0:t0 + nt].rearrange('h d t -> d h t')
                        s_r = RTst[ss][0:64, :, :]
                        d_v = jb['G'][:, :, j0:j0 + NSUB, :]
                        s_v = Vst[ss][:, :, :, 0:128]
                    K.op('pool', lambda e: e.dma_start(out=d_kt.rearrange('h d t -> d h t'), in_=KTst[ss]),
                         r=RKTst[ss], dsem=st_sem[0][ss])
                    K.op('pool', lambda e: e.dma_start(out=d_n2.rearrange('h d t -> d h t'), in_=N2Tst[ss]),
                         r=RN2Tst[ss], dsem=st_sem[1][ss])
                    K.op('pool', lambda e: e.dma_start(out=d_r, in_=s_r), r=RRTst[ss], dsem=st_sem[2][ss])
                    for sb_ in range(NSUB):
                        K.op('pool', lambda e, sb_=sb_: e.dma_start(
                            out=d_v[:, :, sb_, :].rearrange('h p e -> p h e'), in_=s_v[:, sb_, :, :]),
                            r=[RVstA[ss][sb_], RVstB[ss][sb_]], dsem=st_sem[3 + sb_][ss])

            for i0 in range(min(3, NT)):
                ld(i0)
            pre(0)
            if NT > 1:
                pre(1)
            T1(0)
            for i in range(NT + 4):
                if i + 2 < NT:
                    pre(i + 2)
                if i + 1 < NT:
                    T1(i + 1)
                if 0 <= i - 4 < NT:
                    W(i - 4)
                if i < NT:
                    M(i)
                if 0 <= i - 1 < NT:
                    Y(i - 1)
                if 0 <= i - 2 < NT:
                    Z(i - 2)
                if i + 3 < NT:
                    ld(i + 3)
            sup_ctr[0] += NT // NSUB

        if KSTOP >= 2:
            for j in range(2):
                proj_pass(j, 'kv')
                if KSTOP >= 3:
                    proj_pass(j, 'q')
            K.barrier()

        def attn(j):
            jb = jobs[j]
            S, NQ = jb['S'], jb['NQ']
            NKB = S // 128
            HK = NKB // 2
            A.off = 0
            identb = A.alloc([128], BF16)
            KTb = [A.alloc([S], BF16) for _ in range(2)]
            Vb = [A.alloc([NKB, 129], BF16) for _ in range(2)]
            KrTp = A.alloc([S // 2], BF16)
            QA = [A.alloc([1024], BF16) for _ in range(2)]
            QB = [A.alloc([3, 1024], BF16) for _ in range(2)]
            Gb = [A.alloc([8, 128], BF16) for _ in range(2)]
            pT = [A.alloc([1024], BF16) for _ in range(3)]
            accsb = [A.alloc([9, 129], F32) for _ in range(2)]
            rr = [A.alloc([8], F32) for _ in range(2)]
            r2 = [A.alloc([4], F32) for _ in range(2)]
            ss4 = [A.alloc([4], F32) for _ in range(2)]
            lnv = [A.alloc([4], F32) for _ in range(2)]
            rs4 = [A.alloc([4], F32) for _ in range(2)]
            oo = [A.alloc([8, 128], F32) for _ in range(2)]
            tt = A.alloc([4, 128], F32)
            sq = A.alloc([4, 128], F32)
            yb = [A.alloc([8, 128], BF16) for _ in range(2)]
            gsub = A.alloc([128], F32)
            gsubf = A.alloc([128], F32)
            lamt = A.alloc([256], F32)
            lprod = A.alloc([128], F32)
            lsum = A.alloc([2], F32)
            lexp = A.alloc([2], F32)
            ldif = A.alloc([1], F32)
            neglam = A.alloc([1], F32)
            RKT = [Res(), Res()]
            RV = [Res(), Res()]
            RKr = Res()
            RKr2 = Res()
            RQ = [[Res(), Res(), Res()] for _ in range(2)]
            RG = [Res(), Res()]
            RpT = [Res(), Res(), Res()]
            Rsc = [Res(), Res()]
            Racc = Res()
            Raccsb = [Res(), Res()]
            Rrr = [Res(), Res()]
            Rr2 = [Res(), Res()]
            Rss4 = [Res(), Res()]
            Rlnv = [Res(), Res()]
            Rrs4 = [Res(), Res()]
            Roo = [Res(), Res()]
            Rtt, Rsq = Res(), Res()
            Ryb = [Res(), Res()]
            Rc = Res()
            Rgsub, Rlamt, Rlprod, Rlsum, Rlexp, Rldif = (Res() for _ in range(6))
            kt_sem = [K.newsem('kt%d_%d' % (j, i)) for i in range(2)]
            v_sem = [K.newsem('v%d_%d' % (j, i)) for i in range(2)]
            kr_sem = [K.newsem('kr%d_%d' % (j, i)) for i in range(2)]
            q_sem = [[K.newsem('q%d_%d_%d' % (j, i, k)) for k in range(3)] for i in range(2)]
            g_sem = [K.newsem('g%d_%d' % (j, i)) for i in range(2)]
            y_sem = [K.newsem('y%d_%d' % (j, i)) for i in range(2)]
            c_sem = [K.newsem('c%d_%d' % (j, i)) for i in range(2)]

            K.op('sp', lambda e: e.dma_start(out=gsub, in_=gsub_d), w=[Rgsub], dsem=c_sem[0])
            K.op('sp', lambda e: e.dma_start(out=lamt, in_=lam_d), w=[Rlamt], dsem=c_sem[1])
            K.op('dve', lambda e: e.tensor_scalar(out=gsubf, in0=gsub, scalar1=1.0 - LAMBDA_INIT, scalar2=None,
                                                  op0=ALU.mult), r=[Rgsub, Rlamt], w=[Rc])
            K.op('dve', lambda e: e.tensor_tensor(out=lprod, in0=lamt[:, 0:128], in1=lamt[:, 128:256], op=ALU.mult),
                 r=[Rlamt, Rgsub], w=[Rlprod])
            K.op('dve', lambda e: e.tensor_reduce(out=lsum, in_=lprod.rearrange('p (a b) -> p a b', b=64),
                                                  axis=AX.X, op=ALU.add), r=[Rlprod], w=[Rlsum])
            K.op('act', lambda e: e.activation(out=lexp, in_=lsum, func=AF.Exp), r=[Rlsum], w=[Rlexp])
            K.op('dve', lambda e: e.tensor_tensor(out=ldif, in0=lexp[:, 1:2], in1=lexp[:, 0:1], op=ALU.subtract),
                 r=[Rlexp], w=[Rldif])
            K.op('dve', lambda e: e.tensor_scalar(out=neglam, in0=ldif, scalar1=-LAMBDA_INIT, scalar2=None,
                                                  op0=ALU.add), r=[Rldif], w=[Rc])
            for s in range(2):
                K.op('pool', lambda e, s=s: e.memset(QA[s][64:128, 0:512], 0.0), w=[RQ[s][0]])
                K.op('pool', lambda e, s=s: e.memset(QA[s][0:64, 512:1024], 0.0), w=[RQ[s][1]])
                K.op('pool', lambda e, s=s: e.memset(QB[s][64:128, 1, :], 0.0), w=[RQ[s][1]])
                K.op('pool', lambda e, s=s: e.memset(QB[s][0:64, 2, :], 0.0), w=[RQ[s][2]])

            segs = []
            for h in range(8):
                qn = 512 if h < 4 else 1024
                for qc in range(NQ // qn):
                    segs.append((h, qc, qn))

            def load_head(h):
                b = h % 2
                K.op('sp', lambda e: e.dma_start(out=KTb[b], in_=jb['KT'][h]), w=[RKT[b]], dsem=kt_sem[b])
                K.op('sp', lambda e: e.dma_start(out=Vb[b], in_=jb['V'][h]), w=[RV[b]], dsem=v_sem[b])

            def load_seg(g):
                h, qc, qn = segs[g]
                s = g % 2
                q0 = qc * qn
                if h < 4:
                    K.op('sp', lambda e: e.dma_start(out=QA[s][0:64, 0:512], in_=jb['QT'][h, 0:64, q0:q0 + 512]),
                         w=[RQ[s][0]], dsem=q_sem[s][0])
                    K.op('sp', lambda e: e.dma_start(out=QA[s][64:128, 512:1024],
                                                     in_=jb['QT'][h, 64:128, q0:q0 + 512]),
                         w=[RQ[s][1]], dsem=q_sem[s][1])
                    K.op('sp', lambda e: e.dma_start(out=Gb[s][:, 0:4, :],
                                                     in_=jb['G'][h, :, q0 // 128:q0 // 128 + 4, :]),
                         w=[RG[s]], dsem=g_sem[s])
                else:
                    K.op('sp', lambda e: e.dma_start(out=QB[s][:, 0, :], in_=jb['QT'][h, :, q0:q0 + 1024]),
                         w=[RQ[s][0]], dsem=q_sem[s][0])
                    K.op('sp', lambda e: e.dma_start(out=QB[s][0:64, 1, :], in_=jb['QrT'][h - 4, :, q0:q0 + 1024]),
                         w=[RQ[s][1]], dsem=q_sem[s][1])
                    K.op('sp', lambda e: e.dma_start(out=QB[s][64:128, 2, :],
                                                     in_=jb['QrT'][h - 4, :, q0:q0 + 1024]),
                         w=[RQ[s][2]], dsem=q_sem[s][2])
                    K.op('sp', lambda e: e.dma_start(out=Gb[s][:, 0:8, :],
                                                     in_=jb['G'][h, :, q0 // 128:q0 // 128 + 8, :]),
                         w=[RG[s]], dsem=g_sem[s])


            tiles = []
            for g, (h, qc, qn) in enumerate(segs):
                for kb in range(NKB):
                    tiles.append((g, kb))
            T = len(tiles)
            sc = [PS[:, 0:1024], PS[:, 1024:2048]]

            def accap(idx):
                bank, pos = idx // 3, idx % 3
                return psf(4 + bank, 129, pos * 129)

            def QK(t):
                g, kb = tiles[t]
                h, qc, qn = segs[g]
                b, s = h % 2, g % 2
                so = sc[t % 2]
                lk = KTb[b][:, kb * 128:(kb + 1) * 128]
                if h < 4:
                    for hf in range(2):
                        K.op('pe', lambda e, hf=hf: e.matmul(so[:, hf * 512:(hf + 1) * 512], lhsT=lk,
                                                             rhs=QA[s][:, hf * 512:(hf + 1) * 512],
                                                             start=True, stop=True),
                             r=[RKT[b]] + RQ[s], w=[Rsc[t % 2]])
                else:
                    kr = KrTp[:, (kb % HK) * 128:(kb % HK + 1) * 128]
                    qi = 1 if kb < HK else 2
                    for hf in range(2):
                        K.op('pe', lambda e, hf=hf: e.matmul(so[:, hf * 512:(hf + 1) * 512], lhsT=lk,
                                                             rhs=QB[s][:, 0, hf * 512:(hf + 1) * 512],
                                                             start=True, stop=False),
                             r=[RKT[b]] + RQ[s], w=[Rsc[t % 2]])
                        K.op('pe', lambda e, hf=hf: e.matmul(so[:, hf * 512:(hf + 1) * 512], lhsT=kr,
                                                             rhs=QB[s][:, qi, hf * 512:(hf + 1) * 512],
                                                             start=False, stop=True),
                             r=[RKr, RKr2] + RQ[s], w=[Rsc[t % 2]])

            def EX(t):
                g, kb = tiles[t]
                h = segs[g][0]
                scale = 64 ** -0.5 if h < 4 else 192 ** -0.5
                K.op('act', lambda e: e.activation(out=pT[t % 3], in_=sc[t % 2], func=AF.Exp, scale=scale),
                     r=[Rsc[t % 2]], w=[RpT[t % 3]])

            def PV(t):
                g, kb = tiles[t]
                h = segs[g][0]
                b = h % 2
                for a in range(8):
                    K.op('pe', lambda e, a=a: e.matmul(accap(a), lhsT=pT[t % 3][:, a * 128:(a + 1) * 128],
                                                       rhs=Vb[b][:, kb, :], start=(kb == 0 and a % 3 == 0),
                                                       stop=(kb == NKB - 1), skip_group_check=True),
                         r=[RpT[t % 3], RV[b]], w=[Racc])

            deferred = []

            def EP1(g):
                h, qc, qn = segs[g]
                s, p = g % 2, g % 2
                A9 = accsb[p]
                for bk in range(3):
                    n = 387 if bk < 2 else 258
                    K.op('dve', lambda e, bk=bk, n=n: e.tensor_copy(
                        out=A9.rearrange('p a b -> p (a b)')[:, bk * 387:bk * 387 + n], in_=psf(4 + bk, n)),
                        r=[Racc], w=[Raccsb[p]])
                K.op('dve', lambda e: e.reciprocal(out=rr[p], in_=A9[:, 0:8, 128]), r=[Raccsb[p]], w=[Rrr[p]])
                if h < 4:
                    K.op('dve', lambda e: e.tensor_scalar(out=r2[p], in0=rr[p][:, 4:8], scalar1=neglam[:, 0:1],
                                                          scalar2=None, op0=ALU.mult), r=[Rrr[p], Rc], w=[Rr2[p]])
                    K.op('dve', lambda e: e.tensor_tensor(out=tt, in0=A9[:, 4:8, 0:128],
                                                          in1=mk(r2[p][:, 0:1], [(1, 4), (0, 128)]), op=ALU.mult),
                         r=[Raccsb[p], Rr2[p]], w=[Rtt])
                    o4 = oo[p][:, 0:4, :]
                    K.op('dve', lambda e: e.tensor_tensor(out=o4, in0=A9[:, 0:4, 0:128],
                                                          in1=mk(rr[p][:, 0:1], [(1, 4), (0, 128)]), op=ALU.mult),
                         r=[Raccsb[p], Rrr[p]], w=[Roo[p]])
                    K.op('dve', lambda e: e.tensor_tensor(out=o4, in0=o4, in1=tt, op=ALU.add),
                         r=[Roo[p], Rtt], w=[Roo[p]])
                    K.op('pool', lambda e: e.tensor_tensor(out=sq, in0=o4, in1=o4, op=ALU.mult),
                         r=[Roo[p]], w=[Rsq])
                    K.op('dve', lambda e: e.tensor_reduce(out=ss4[p], in_=sq, axis=AX.X, op=ALU.add),
                         r=[Rsq], w=[Rss4[p]])
                else:
                    K.op('dve', lambda e: e.tensor_tensor(out=oo[p], in0=A9[:, 0:8, 0:128],
                                                          in1=mk(rr[p][:, 0:1], [(1, 8), (0, 128)]), op=ALU.mult),
                         r=[Raccsb[p], Rrr[p]], w=[Roo[p]])

            def EP2(g):
                h, qc, qn = segs[g]
                s, p = g % 2, g % 2
                q0 = qc * qn
                nsb = qn // 128
                if h < 4:
                    o4 = oo[p][:, 0:4, :]
                    K.op('act', lambda e: e.activation(out=lnv[p], in_=ss4[p], func=AF.Ln, scale=1.0 / 128,
                                                       bias=EPS), r=[Rss4[p]], w=[Rlnv[p]])
                    K.op('act', lambda e: e.activation(out=rs4[p], in_=lnv[p], func=AF.Exp, scale=-0.5),
                         r=[Rlnv[p]], w=[Rrs4[p]])
                    K.op('dve', lambda e: e.tensor_tensor(out=o4, in0=o4,
                                                          in1=mk(rs4[p][:, 0:1], [(1, 4), (0, 128)]), op=ALU.mult),
                         r=[Roo[p], Rrs4[p]], w=[Roo[p]])
                    K.op('pool', lambda e: e.tensor_tensor(out=o4, in0=o4,
                                                           in1=mk(gsubf[:, 0:1], [(0, 4), (1, 128)]), op=ALU.mult),
                         r=[Roo[p], Rc], w=[Roo[p]])
                    K.op('dve', lambda e: e.tensor_tensor(out=yb[p][:, 0:4, :], in0=o4, in1=Gb[s][:, 0:4, :],
                                                          op=ALU.mult), r=[Roo[p], RG[s]], w=[Ryb[p]])
                else:
                    K.op('pool', lambda e: e.tensor_tensor(out=yb[p], in0=oo[p], in1=Gb[s], op=ALU.mult),
                         r=[Roo[p], RG[s]], w=[Ryb[p]])
                ybase = jb['Y'][q0:q0 + 128, h * 128:(h + 1) * 128]
                dst = mk(ybase, [(128 * D, nsb), (1, 128)])
                K.op('pool', lambda e: e.dma_start(out=dst, in_=yb[p][:, 0:nsb, :]), r=[Ryb[p]], dsem=y_sem[p])

            load_head(0)
            load_seg(0)
            if len(segs) > 1:
                load_seg(1)
            load_head(1)
            kr_loaded = [False]

            def load_kr():
                K.op('sp', lambda e: e.dma_start(out=KrTp[0:64, :], in_=jb['KrT'][:, 0:S // 2]), w=[RKr],
                     dsem=kr_sem[0])
                K.op('sp', lambda e: e.dma_start(out=KrTp[64:128, :], in_=jb['KrT'][:, S // 2:S]), w=[RKr2],
                     dsem=kr_sem[1])
            load_kr()
            QK(0)
            if T > 1:
                QK(1)
            for t in range(T):
                g, kb = tiles[t]
                h = segs[g][0]
                EX(t)
                if t + 2 < T:
                    QK(t + 2)
                PV(t)
                for d in list(deferred):
                    d[0] -= 1
                    if d[0] <= 0:
                        deferred.remove(d)
                        d[1]()
                if kb == NKB - 1:
                    EP1(g)
                    deferred.append([4, lambda g=g: EP2(g)])
                    if g + 2 < len(segs):
                        deferred.append([6, lambda g=g: load_seg(g + 2)])
                    if (g + 1 == len(segs) or segs[g + 1][0] != h) and h + 2 < 8:
                        load_head(h + 2)
            for d in deferred:
                d[1]()
            K.barrier()

        if KSTOP >= 4:
            attn(0)
        if KSTOP >= 5:
            attn(1)

        A.off = 0
        identc = A.alloc([128], BF16)
        w_out_sb = A.alloc([8, D], BF16)
        ggp = [A.alloc([D], F32) for _ in range(2)]
        wst2 = [A.alloc([D], F32) for _ in range(2)]
        yt = [A.alloc([D], BF16) for _ in range(2)]
        xt = [A.alloc([D], F32) for _ in range(2)]
        yT = [A.alloc([8, 128], BF16) for _ in range(2)]
        t1 = [A.alloc([D], F32) for _ in range(2)]
        res = [A.alloc([D], F32) for _ in range(2)]
        junk2 = A.alloc([D], BF16)
        so = [A.alloc([8], F32) for _ in range(4)]
        Ridc, Rwout = Res(), Res()
        Rjunk2 = [Res(), Res()]
        Rggp = [Res(), Res()]
        Rwst2 = [Res(), Res()]
        Ryt = [Res(), Res()]
        Rxt = [Res(), Res()]
        RyT = [Res(), Res()]
        Rt1 = [Res(), Res()]
        Rres = [Res(), Res()]
        Rso = [[Res() for _ in range(5)] for _ in range(4)]
        RpsT2 = Res()
        RpO = [Res(), Res()]
        o_ws = [K.newsem('ows0'), K.newsem('ows1')]
        o_y = [K.newsem('oy0'), K.newsem('oy1')]
        o_x = [K.newsem('ox0'), K.newsem('ox1')]
        o_st = [K.newsem('ost0'), K.newsem('ost1')]
        o_c = K.newsem('oc')
        K.op('sp', lambda e: e.dma_start(out=identc, in_=ident_d), w=[Ridc], dsem=o_c)
        for j in range(2):
            K.op('sp', lambda e, j=j: e.dma_start(out=ggp[j], in_=jobs[j]['GGP']), w=[Rggp[j]],
                 dsem=K.newsem('oggp%d' % j))
        for k in range(8):
            s = k % 2
            K.op('sp', lambda e, s=s, k=k: e.dma_start(out=wst2[s], in_=wout_d[k * 128:(k + 1) * 128, :]),
                 w=[Rwst2[s]], dsem=o_ws[s])
            K.op('dve' if k % 2 == 0 else 'pool', lambda e, s=s, k=k: e.tensor_copy(out=w_out_sb[:, k, :],
                                                                                    in_=wst2[s]),
                 r=[Rwst2[s]], w=[Rwout])
        psT2 = psb(0)
        pO = [PS[:, 1024:2048], PS[:, 2048:3072]]
        it = [0]
        out_final = []
        for j in range(2 if KSTOP >= 6 else 0):
            jb = jobs[j]
            for i in range(jb['NQ'] // 128):
                n = it[0]
                it[0] += 1
                s2, s4 = n % 2, n % 4
                rows = slice(i * 128, (i + 1) * 128)
                K.op('sp', lambda e, s2=s2, rows=rows, jb=jb: e.dma_start(out=yt[s2], in_=jb['Y'][rows, :]),
                     w=[Ryt[s2]], dsem=o_y[s2])
                K.op('sp', lambda e, s2=s2, rows=rows, jb=jb: e.dma_start(out=xt[s2], in_=jb['xq'][rows, :]),
                     w=[Rxt[s2]], dsem=o_x[s2])
                for c in range(8):
                    K.op('pe', lambda e, c=c, s2=s2: e.transpose(psT2[:, c * 128:(c + 1) * 128],
                                                                yt[s2][:, c * 128:(c + 1) * 128], identc),
                         r=[Ryt[s2], Ridc], w=[RpsT2])
                K.op('act', lambda e, s2=s2: e.activation(out=yT[s2].rearrange('p a b -> p (a b)'), in_=psT2,
                                                          func=AF.Copy), r=[RpsT2], w=[RyT[s2]])
                for hf in range(2):
                    for c in range(8):
                        K.op('pe', lambda e, c=c, hf=hf, s2=s2: e.matmul(
                            pO[s2][:, hf * 512:(hf + 1) * 512], lhsT=yT[s2][:, c, :],
                            rhs=w_out_sb[:, c, hf * 512:(hf + 1) * 512], start=(c == 0), stop=(c == 7)),
                            r=[RyT[s2], Rwout], w=[RpO[s2]])
                st = so[s4]
                for hf in range(2):
                    K.op('act', lambda e, hf=hf, s2=s2, st=st: e.activation(
                        out=junk2[:, hf * 512:(hf + 1) * 512], in_=pO[s2][:, hf * 512:(hf + 1) * 512], func=AF.Square,
                        accum_out=st[:, hf:hf + 1]), r=[RpO[s2]], w=[Rso[s4][hf], Rjunk2[hf]])
                K.op('dve', lambda e, st=st: e.tensor_tensor(out=st[:, 2:3], in0=st[:, 0:1], in1=st[:, 1:2],
                                                             op=ALU.add), r=[Rso[s4][0], Rso[s4][1]], w=[Rso[s4][2]])
                K.op('act', lambda e, st=st: e.activation(out=st[:, 3:4], in_=st[:, 2:3], func=AF.Sqrt,
                                                          scale=1.0 / D, bias=EPS), r=[Rso[s4][2]], w=[Rso[s4][3]])
                K.op('dve', lambda e, st=st: e.reciprocal(out=st[:, 4:5], in_=st[:, 3:4]), r=[Rso[s4][3]],
                     w=[Rso[s4][4]])
                for hf in range(2):
                    K.op('dve', lambda e, st=st, s2=s2, j=j, hf=hf: e.scalar_tensor_tensor(
                        out=t1[s2][:, hf * 512:(hf + 1) * 512], in0=pO[s2][:, hf * 512:(hf + 1) * 512],
                        scalar=st[:, 4:5], in1=ggp[j][:, hf * 512:(hf + 1) * 512], op0=ALU.mult, op1=ALU.mult),
                        r=[RpO[s2], Rso[s4][4], Rggp[j]], w=[Rt1[s2]])
                K.op('pool', lambda e, s2=s2: e.tensor_tensor(out=res[s2], in0=t1[s2], in1=xt[s2], op=ALU.add),
                     r=[Rt1[s2], Rxt[s2]], w=[Rres[s2]])
                K.op('pool', lambda e, s2=s2, rows=rows, jb=jb: e.dma_start(out=jb['y'][rows, :], in_=res[s2]),
                     r=[Rres[s2]], dsem=o_st[s2])

        with nc.Block() as block:
            K.finalize(block)
    return nc


def rope_table(pos):
    theta = np.float32(500000.0)
    out = np.zeros((len(pos), 160), np.float32)
    p = pos.astype(np.float32)[:, None]
    for dim, c0, s0, half in ((16, 0, 16, 8), (64, 32, 96, 32)):
        inv = theta ** (-np.arange(0, dim, 2, dtype=np.float32) / np.float32(dim))
        ang = (p * inv[None, :].astype(np.float32)).astype(np.float32)
        out[:, c0:c0 + half] = np.cos(ang).astype(np.float32)
        out[:, c0 + half:c0 + 2 * half] = out[:, c0:c0 + half]
        out[:, s0:s0 + half] = np.sin(ang).astype(np.float32)
        out[:, s0 + half:s0 + 2 * half] = out[:, s0:s0 + half]
    return out


_NC_CACHE = {}


def kernel(x_prompt, x_sample, c_prompt, c_sample, w_ada, b_ada, g_pre, w_in, lambda_q1, lambda_k1,
           lambda_q2, lambda_k2, g_subln, g_cq, w_uq, g_ckv, w_ukv, w_out, g_post):
    f = lambda a: np.ascontiguousarray(np.asarray(a, dtype=np.float32))
    x_prompt, x_sample = f(x_prompt), f(x_sample)
    B, SPr, _ = x_prompt.shape
    DB, SSm, _ = x_sample.shape
    ncore = 8
    assert B == ncore
    per = ncore // DB
    NQS = SSm // per
    key = (SPr, SSm, NQS)
    if key not in _NC_CACHE:
        _NC_CACHE[key] = build(SPr, SSm, NQS)
    nc = _NC_CACHE[key]
    col = lambda v: np.ascontiguousarray(f(v).reshape(-1, 128).T)
    bc = lambda v: np.ascontiguousarray(np.broadcast_to(f(v).reshape(1, -1), (128, f(v).size)))
    tabP = rope_table(np.arange(SPr))
    tabS = rope_table(np.arange(SSm))
    gcol = np.zeros((128, 8), np.float32)
    gcol[:, 0:3] = col(g_cq[0])
    gcol[:, 3:5] = col(g_ckv[0])
    lam = np.concatenate([f(lambda_q1[0]), f(lambda_q2[0]), f(lambda_k1[0]), f(lambda_k2[0])])
    shared = {
        'w_ada': f(w_ada[0]), 'bada_bc': bc(b_ada[0]), 'gpre_bc': bc(g_pre[0]), 'gpost_bc': bc(g_post[0]),
        'gsub_bc': bc(g_subln[0]), 'lam_bc': bc(lam), 'gcol': gcol, 'w_in': f(w_in[0]), 'w_uq': f(w_uq[0]),
        'w_ukv': f(w_ukv[0]), 'w_out': f(w_out[0]), 'ident': np.eye(128).astype(ml_dtypes.bfloat16),
        'tabP': tabP, 'tabS': tabS,
    }
    c_prompt, c_sample = f(c_prompt), f(c_sample)
    in_maps = []
    for c in range(ncore):
        sidx, r = c // per, c % per
        m = dict(shared)
        m['xkvP'] = x_prompt[c]
        m['xkvS'] = x_sample[sidx]
        m['xqS'] = np.ascontiguousarray(x_sample[sidx, r * NQS:(r + 1) * NQS])
        m['tabqS'] = np.ascontiguousarray(tabS[r * NQS:(r + 1) * NQS])
        m['ccol'] = np.ascontiguousarray(np.concatenate([col(c_prompt[c]), col(c_sample[sidx])], axis=1))
        in_maps.append(m)
    res = run_bass_kernel_spmd(nc, in_maps, core_ids=list(range(ncore)))
    yP = np.stack([np.asarray(res.results[c]['yP'], dtype=np.float32) for c in range(ncore)], axis=0)
    yS = np.zeros_like(x_sample)
    for c in range(ncore):
        sidx, r = c // per, c % per
        yS[sidx, r * NQS:(r + 1) * NQS] = np.asarray(res.results[c]['yS'], dtype=np.float32)
    return (yP, yS)
```

```python
import numpy as np
import ml_dtypes
import concourse.bass as bass
import concourse.mybir as mybir
from concourse.ap import AP
from concourse.bass_utils import run_bass_kernel_spmd
from contextlib import ExitStack

F32 = mybir.dt.float32
BF16 = mybir.dt.bfloat16
AF = mybir.ActivationFunctionType
ALU = mybir.AluOpType
AX = mybir.AxisListType

D = 1024
C_AQ, C_AK, C_AV, C_AG, C_CQ, C_CKV, C_KR, C_BG = 0, 512, 1024, 1536, 2048, 2432, 2688, 2752
INC = 3264
EPS = 1e-6
NSUB = 2
NST = 3
ENGS = ('pe', 'act', 'dve', 'pool', 'sp')
LAMBDA_INIT = 0.8 - 0.6 * 1.0
SBUF_WORDS = 207 * 256


class Sem:
    __slots__ = ('h', 'n', 'name')

    def __init__(self, h, name):
        self.h = h
        self.n = 0
        self.name = name


class Op:
    __slots__ = ('eng', 'fn', 'deps', 'sem', 'inc', 'flag', 'cnt', 'isdma')


class Res:
    __slots__ = ('w', 'r')

    def __init__(self):
        self.w = None
        self.r = {}


class Kern:
    def __init__(self, nc, es):
        self.nc = nc
        self.es = es
        self.ops = {e: [] for e in ENGS}
        self.esem = {e: self.newsem('e_' + e) for e in ENGS}
        self.bar_deps = []
        self.bar_pending = set()
        self.dma_out = []
        self.nsem = 0

    def newsem(self, name):
        h = self.es.enter_context(self.nc.semaphore(name))
        return Sem(h, name)

    def op(self, eng, fn, r=(), w=(), dsem=None):
        o = Op()
        o.eng = eng
        o.fn = fn
        o.isdma = dsem is not None
        o.sem = dsem if dsem is not None else self.esem[eng]
        o.inc = 16 if o.isdma else 1
        o.flag = o.isdma
        o.cnt = None
        deps = {}
        for x in r:
            if x.w is not None:
                deps[id(x.w)] = x.w
        for x in w:
            xw = x.w
            if xw is not None and (o.isdma or xw.isdma or xw.eng != eng or eng != 'pe'):
                deps[id(xw)] = xw
            for rd in x.r.values():
                if o.isdma or rd.isdma or rd.eng != eng or eng != 'pe':
                    deps[id(rd)] = rd
        if eng in self.bar_pending:
            self.bar_pending.discard(eng)
            for d in self.bar_deps:
                deps[id(d)] = d
        o.deps = list(deps.values())
        for x in r:
            x.r[('d', id(o)) if o.isdma else eng] = o
        for x in w:
            x.w = o
            x.r = {}
        self.ops[eng].append(o)
        if o.isdma:
            self.dma_out.append(o)
        return o

    def barrier(self):
        deps = []
        for e in ENGS:
            for o in reversed(self.ops[e]):
                if not o.isdma:
                    deps.append(o)
                    break
        deps.extend(self.dma_out)
        self.dma_out = []
        self.bar_deps = deps
        self.bar_pending = set(ENGS)

    def finalize(self, block):
        for e in ENGS:
            for o in self.ops[e]:
                for d in o.deps:
                    d.flag = True
        final = list(self.dma_out)
        for e in ENGS:
            for o in self.ops[e]:
                if o.flag:
                    o.sem.n += o.inc
                    o.cnt = o.sem.n
        maxc = max(s.n for s in self.esem.values())
        assert maxc < 60000, maxc

        def replay(eng_name, e):
            waited = {}
            for o in self.ops[eng_name]:
                for d in o.deps:
                    s = d.sem
                    if waited.get(id(s), 0) < d.cnt:
                        e.wait_ge(s.h, d.cnt)
                        waited[id(s)] = d.cnt
                ins = o.fn(e)
                if o.flag:
                    ins.then_inc(o.sem.h, o.inc)
            if eng_name == 'sp':
                for o in final:
                    if waited.get(id(o.sem), 0) < o.cnt:
                        e.wait_ge(o.sem.h, o.cnt)
                        waited[id(o.sem)] = o.cnt

        @block.tensor
        def _(t):
            replay('pe', t)

        @block.scalar
        def _(a):
            replay('act', a)

        @block.vector
        def _(v):
            replay('dve', v)

        @block.gpsimd
        def _(g):
            replay('pool', g)

        @block.sync
        def _(s):
            replay('sp', s)


class Arena:
    def __init__(self, t):
        self.t = t
        self.off = 0

    def alloc(self, shape, dt, parts=128):
        free = 1
        for s in shape:
            free *= s
        nb = free * (2 if dt == BF16 else 4)
        words = (nb + 3) // 4
        words = (words + 7) // 8 * 8
        st = self.off
        self.off += words
        assert self.off <= SBUF_WORDS, (self.off, SBUF_WORDS)
        v = self.t[0:parts, st:st + words]
        if dt == BF16:
            v = v.bitcast(BF16)
        v = v[:, 0:free]
        return reshape(v, shape)


def reshape(v, shape):
    if len(shape) == 1:
        return v
    names = 'abcde'[:len(shape)]
    kw = {names[i]: shape[i] for i in range(1, len(shape))}
    return v.rearrange('p (' + ' '.join(names) + ') -> p ' + ' '.join(names), **kw)


def mk(base, dims):
    return AP(base.tensor, base.offset, [list(base.ap[0])] + [[int(s), int(c)] for s, c in dims])


def build(SPr, SSm, NQS):
    nc = bass.Bass("TRN2", target_bir_lowering=False)

    def din(name, shape, dt=F32):
        return nc.dram_tensor(name, list(shape), dt, kind="ExternalInput").ap()

    def dout(name, shape, dt=F32):
        return nc.dram_tensor(name, list(shape), dt, kind="ExternalOutput").ap()

    def dscr(name, shape, dt=BF16):
        return nc.dram_tensor(name, list(shape), dt, kind="Internal").ap()

    jobs = []
    for jn, S, NQ in (('P', SPr, SPr), ('S', SSm, NQS)):
        jb = dict(S=S, NQ=NQ, name=jn)
        jb['xkv'] = din('xkv' + jn, [S, D])
        jb['tab'] = din('tab' + jn, [S, 160])
        if jn == 'P':
            jb['xq'] = jb['xkv']
            jb['tabq'] = jb['tab']
        else:
            jb['xq'] = din('xq' + jn, [NQ, D])
            jb['tabq'] = din('tabq' + jn, [NQ, 160])
        jb['y'] = dout('y' + jn, [NQ, D])
        jb['KT'] = dscr('KT' + jn, [8, 128, S])
        jb['KrT'] = dscr('KrT' + jn, [64, S])
        jb['V'] = dscr('V' + jn, [8, 128, S // 128, 129])
        jb['QT'] = dscr('QT' + jn, [8, 128, NQ])
        jb['QrT'] = dscr('QrT' + jn, [4, 64, NQ])
        jb['G'] = dscr('G' + jn, [8, 128, NQ // 128, 128])
        jb['Y'] = dscr('Y' + jn, [NQ, D])
        jb['GGP'] = dscr('GGP' + jn, [128, D], F32)
        jobs.append(jb)
    ccol_d = din('ccol', [128, 16])
    wada_d = din('w_ada', [D, 3 * D])
    bada_d = din('bada_bc', [128, 3 * D])
    gpre_d = din('gpre_bc', [128, D])
    gpost_d = din('gpost_bc', [128, D])
    gsub_d = din('gsub_bc', [128, 128])
    lam_d = din('lam_bc', [128, 256])
    gcol_d = din('gcol', [128, 8])
    win_d = din('w_in', [D, INC])
    wuq_d = din('w_uq', [384, 768])
    wukv_d = din('w_ukv', [256, 1024])
    wout_d = din('w_out', [D, D])
    ident_d = din('ident', [128, 128], BF16)

    with ExitStack() as es:
        sb_t = es.enter_context(nc.sbuf_tensor("arena", [128, SBUF_WORDS], F32))
        ps_t = es.enter_context(nc.psum_tensor("psarena", [128, 4096], F32))
        K = Kern(nc, es)
        PS = ps_t[:, :]

        def psf(bank, n=512, off=0):
            return PS[:, bank * 512 + off: bank * 512 + off + n]

        def psb(bank, n=1024, off=0, parts=128):
            v = PS[0:parts, bank * 512: (bank + 1) * 512].bitcast(BF16)
            return v[:, off:off + n]

        A = Arena(sb_t[:, :])
        ident = A.alloc([128], BF16)
        w_in_sb = A.alloc([8, INC], BF16)
        w_uq_sb = A.alloc([3, 768], BF16)
        w_ukv_sb = A.alloc([2, 1024], BF16)
        gsb = [A.alloc([D], F32) for _ in range(2)]
        shb = [A.alloc([D], F32) for _ in range(2)]
        gcol = A.alloc([8], F32)
        Rident, Rwin, Rwuq, Rwukv, Rgcol = Res(), Res(), Res(), Res(), Res()
        Rgs = [Res(), Res()]
        Rsh = [Res(), Res()]
        mark_proj = A.off

        K.op('sp', lambda e: e.dma_start(out=ident, in_=ident_d), w=[Rident], dsem=K.newsem('ld_id'))
        K.op('sp', lambda e: e.dma_start(out=gcol, in_=gcol_d), w=[Rgcol], dsem=K.newsem('ld_gc'))

        bada = A.alloc([3 * D], F32)
        gpre = A.alloc([D], F32)
        gpost = A.alloc([D], F32)
        ccol = A.alloc([16], F32)
        scv = A.alloc([16], F32)
        screp = A.alloc([8, 128], F32)
        wst = [A.alloc([INC], F32) for _ in range(2)]
        tmpm = A.alloc([D], F32)
        tmpg = A.alloc([D], F32)
        Rbada, Rgpre, Rgpost, Rccol, Rscv, Rscrep, Rtmpm, Rtmpg = (Res() for _ in range(8))
        Rwst = [Res(), Res()]
        wst_sem = [K.newsem('wst0'), K.newsem('wst1')]
        Rpm = [Res() for _ in range(8)]
        K.op('sp', lambda e: e.dma_start(out=bada, in_=bada_d), w=[Rbada], dsem=K.newsem('ld_b'))
        K.op('sp', lambda e: e.dma_start(out=gpre, in_=gpre_d), w=[Rgpre], dsem=K.newsem('ld_gp'))
        K.op('sp', lambda e: e.dma_start(out=gpost, in_=gpost_d), w=[Rgpost], dsem=K.newsem('ld_gq'))
        K.op('sp', lambda e: e.dma_start(out=ccol, in_=ccol_d), w=[Rccol], dsem=K.newsem('ld_cc'))
        K.op('act', lambda e: e.activation(out=scv, in_=ccol, func=AF.Silu), r=[Rccol], w=[Rscv])
        wcnt = [0]
        ggp_sem = K.newsem('st_ggp')

        def stage_load(src, ncols):
            s = wcnt[0] % 2
            wcnt[0] += 1
            K.op('sp', lambda e: e.dma_start(out=wst[s][:, 0:ncols], in_=src), w=[Rwst[s]], dsem=wst_sem[s])
            return s

        for j in range(2):
            K.op('dve', lambda e, j=j: e.tensor_copy(out=screp, in_=mk(scv[:, j * 8:j * 8 + 8], [(1, 8), (0, 128)])),
                 r=[Rscv], w=[Rscrep])
            for fc in range(8):
                s = stage_load(wada_d[fc * 128:(fc + 1) * 128, :], 3 * D)
                for cc in range(6):
                    K.op('pe', lambda e, s=s, fc=fc, cc=cc: e.matmul(
                        psf(cc), lhsT=screp[:, fc, :], rhs=wst[s][:, cc * 512:(cc + 1) * 512],
                        start=(fc == 0), stop=(fc == 7)), r=[Rscrep, Rwst[s]], w=[Rpm[cc]])
            for hf in range(2):
                sl = slice(hf * 512, (hf + 1) * 512)
                K.op('dve', lambda e, hf=hf, sl=sl, j=j: e.tensor_tensor(
                    out=shb[j][:, sl], in0=psf(hf), in1=bada[:, sl], op=ALU.add),
                    r=[Rpm[hf], Rbada], w=[Rsh[j]])
                K.op('dve', lambda e, hf=hf, sl=sl: e.tensor_tensor(
                    out=tmpm[:, sl], in0=psf(2 + hf), in1=bada[:, D + hf * 512:D + (hf + 1) * 512], op=ALU.add),
                    r=[Rpm[2 + hf], Rbada], w=[Rtmpm])
                K.op('dve', lambda e, sl=sl, j=j: e.scalar_tensor_tensor(
                    out=gsb[j][:, sl], in0=tmpm[:, sl], scalar=1.0, in1=gpre[:, sl], op0=ALU.add, op1=ALU.mult),
                    r=[Rtmpm, Rgpre], w=[Rgs[j]])
                K.op('dve', lambda e, hf=hf, sl=sl: e.tensor_tensor(
                    out=tmpm[:, sl], in0=psf(4 + hf), in1=bada[:, 2 * D + hf * 512:2 * D + (hf + 1) * 512],
                    op=ALU.add), r=[Rpm[4 + hf], Rbada], w=[Rtmpm])
                K.op('dve', lambda e, sl=sl: e.tensor_tensor(
                    out=tmpg[:, sl], in0=tmpm[:, sl], in1=gpost[:, sl], op=ALU.mult),
                    r=[Rtmpm, Rgpost], w=[Rtmpg])
            K.op('sp', lambda e, j=j: e.dma_start(out=jobs[j]['GGP'], in_=tmpg), r=[Rtmpg], dsem=ggp_sem)
        for k in range(8):
            s = stage_load(win_d[k * 128:(k + 1) * 128, :], INC)
            K.op('dve' if k % 2 == 0 else 'pool', lambda e, s=s, k=k: e.tensor_copy(out=w_in_sb[:, k, :], in_=wst[s]),
                 r=[Rwst[s]], w=[Rwin])
        for k in range(3):
            s = stage_load(wuq_d[k * 128:(k + 1) * 128, :], 768)
            K.op('dve', lambda e, s=s, k=k: e.tensor_scalar(
                out=w_uq_sb[:, k, :], in0=wst[s][:, 0:768], scalar1=gcol[:, k:k + 1], scalar2=None, op0=ALU.mult),
                r=[Rwst[s], Rgcol], w=[Rwuq])
        for k in range(2):
            s = stage_load(wukv_d[k * 128:(k + 1) * 128, :], 1024)
            K.op('dve', lambda e, s=s, k=k: e.tensor_scalar(
                out=w_ukv_sb[:, k, :], in0=wst[s][:, 0:1024], scalar1=gcol[:, 3 + k:4 + k], scalar2=None,
                op0=ALU.mult), r=[Rwst[s], Rgcol], w=[Rwukv])
        K.barrier()
        KSTOP = 9

        A.off = mark_proj
        NX = 4
        xs = [A.alloc([D], F32) for _ in range(NX)]
        NTB = 6
        tabs = [A.alloc([160], F32) for _ in range(NTB)]
        junk = A.alloc([D], BF16)
        stt = [A.alloc([8], F32) for _ in range(4)]
        tmp = [A.alloc([D], F32) for _ in range(2)]
        hb = [A.alloc([D], BF16) for _ in range(2)]
        hT = [A.alloc([8, 128], BF16) for _ in range(2)]
        ktok = [A.alloc([512], BF16) for _ in range(2)]
        ta = [A.alloc([8, 16], F32) for _ in range(2)]
        tb = [A.alloc([8, 16], F32) for _ in range(2)]
        ro = [A.alloc([8, 16], F32) for _ in range(2)]
        Rro = [Res(), Res()]
        c2b = [A.alloc([384], BF16) for _ in range(2)]
        c2T = [A.alloc([384], BF16) for _ in range(2)]
        krtok = [A.alloc([64], BF16) for _ in range(2)]
        tq_a = [A.alloc([4, 64], F32) for _ in range(2)]
        tq_b = [A.alloc([4, 64], F32) for _ in range(2)]
        n2tok = [A.alloc([4, 128], BF16) for _ in range(2)]
        qrtok = [A.alloc([4, 64], BF16) for _ in range(2)]
        KTst = [A.alloc([4, NSUB * 128], BF16) for _ in range(NST)]
        N2Tst = [A.alloc([4, NSUB * 128], BF16) for _ in range(NST)]
        RTst = [A.alloc([4, NSUB * 128], BF16) for _ in range(NST)]
        Vst = [A.alloc([NSUB, 8, 129], BF16) for _ in range(NST)]
        x_sem = [K.newsem('x%d' % i) for i in range(NX)]
        t_sem = [K.newsem('t%d' % i) for i in range(NTB)]
        st_sem = [[K.newsem('st%d_%d' % (a, b)) for b in range(NST)] for a in range(3 + NSUB)]
        Rx = [Res() for _ in range(NX)]
        Rtab = [Res() for _ in range(NTB)]
        Rst = [[Res() for _ in range(6)] for _ in range(4)]
        Rjunk = Res()
        Rtmp = [Res(), Res()]
        Rhb = [Res(), Res()]
        RhT = [Res(), Res()]
        Rktok = [Res(), Res()]
        Rta = [Res(), Res()]
        Rtb = [Res(), Res()]
        Rc2b = [Res(), Res()]
        Rc2T = [Res(), Res()]
        Rkrtok = [Res(), Res()]
        Rtqa = [Res(), Res()]
        Rtqb = [Res(), Res()]
        Rn2tok = [Res(), Res()]
        Rqrtok = [Res(), Res()]
        RKTst = [[Res() for _ in range(NSUB)] for _ in range(NST)]
        RN2Tst = [[Res() for _ in range(NSUB)] for _ in range(NST)]
        RRTst = [[Res() for _ in range(NSUB)] for _ in range(NST)]
        RVstA = [[Res() for _ in range(NSUB)] for _ in range(NST)]
        RVstB = [[Res() for _ in range(NSUB)] for _ in range(NST)]
        RpsT, RpsK, RpsC, RpKV = (Res() for _ in range(4))
        RpP = [Res(), Res(), Res()]
        psT = psb(0)
        pP = [psf(1), psf(2), psf(3)]
        psK = psb(4)
        psC = psb(5)
        psC64 = psb(5, parts=64)
        pKV = PS[:, 6 * 512: 8 * 512]
        bank_ctr = [0]
        sup_ctr = [0]
        for _s in range(NST):
            K.op('pool', lambda e, _s=_s: e.memset(Vst[_s].rearrange('p a b c -> p (a b) c')[:, :, 128:129], 1.0), w=RVstA[_s])

        def proj_pass(j, kind):
            jb = jobs[j]
            kv = (kind == 'kv')
            xsrc, tabsrc = (jb['xkv'], jb['tab']) if kv else (jb['xq'], jb['tabq'])
            NT = (jb['S'] if kv else jb['NQ']) // 128
            KP = 9
            assert NT % NSUB == 0
            if kv:
                chunks = [('rope', C_AK, 512), ('v', C_AV, 512), ('c2', C_CKV, 320)]
                nc2, nk, nout, w2, Rw2 = 256, 2, 1024, w_ukv_sb, Rwukv
            else:
                chunks = [('rope', C_AQ, 512), ('c2', C_CQ, 384), ('ga', C_AG, 512), ('gb', C_BG, 512)]
                nc2, nk, nout, w2, Rw2 = 384, 3, 768, w_uq_sb, Rwuq
            sup0 = sup_ctr[0]

            def slots(i):
                sup = sup0 + i // NSUB
                return i % NX, i % 2, sup % NST, i % NSUB, i % 4

            def ld(i):
                s, s2, ss, sub, s4 = slots(i)
                K.op('sp', lambda e: e.dma_start(out=xs[s], in_=xsrc[i * 128:(i + 1) * 128, :]), w=[Rx[s]],
                     dsem=x_sem[s])
                tbs = i % NTB
                K.op('sp', lambda e: e.dma_start(out=tabs[tbs], in_=tabsrc[i * 128:(i + 1) * 128, :]), w=[Rtab[tbs]],
                     dsem=t_sem[tbs])

            def pre(i):
                s, s2, ss, sub, s4 = slots(i)
                st = stt[s4]
                K.op('act', lambda e: e.activation(out=junk, in_=xs[s], func=AF.Square, accum_out=st[:, 0:1]),
                     r=[Rx[s]], w=[Rst[s4][0], Rjunk])
                K.op('act', lambda e: e.activation(out=st[:, 1:2], in_=st[:, 0:1], func=AF.Sqrt, scale=1.0 / D,
                                                   bias=EPS), r=[Rst[s4][0]], w=[Rst[s4][1]])
                K.op('dve', lambda e: e.reciprocal(out=st[:, 2:3], in_=st[:, 1:2]), r=[Rst[s4][1]], w=[Rst[s4][2]])
                K.op('dve', lambda e: e.scalar_tensor_tensor(out=tmp[s2], in0=xs[s], scalar=st[:, 2:3], in1=gsb[j],
                                                             op0=ALU.mult, op1=ALU.mult),
                     r=[Rx[s], Rst[s4][2], Rgs[j]], w=[Rtmp[s2]])
                K.op('pool', lambda e: e.tensor_tensor(out=hb[s2], in0=tmp[s2], in1=shb[j], op=ALU.add),
                     r=[Rtmp[s2], Rsh[j]], w=[Rhb[s2]])

            def T1(i):
                s, s2, ss, sub, s4 = slots(i)
                for c in range(8):
                    K.op('pe', lambda e, c=c: e.transpose(psT[:, c * 128:(c + 1) * 128],
                                                          hb[s2][:, c * 128:(c + 1) * 128], ident),
                         r=[Rhb[s2], Rident], w=[RpsT])
                K.op('act', lambda e: e.activation(out=hT[s2].rearrange('p a b -> p (a b)'), in_=psT, func=AF.Copy),
                     r=[RpsT], w=[RhT[s2]])

            def M(i):
                s, s2, ss, sub, s4 = slots(i)
                st = stt[s4]
                tab = tabs[i % NTB]
                for (ck, col0, n) in chunks:
                    bk = bank_ctr[0] % 3
                    bank_ctr[0] += 1
                    for c in range(8):
                        K.op('pe', lambda e, c=c, bk=bk, col0=col0, n=n: e.matmul(
                            pP[bk][:, 0:n], lhsT=hT[s2][:, c, :], rhs=w_in_sb[:, c, col0:col0 + n],
                            start=(c == 0), stop=(c == 7)), r=[RhT[s2], Rwin], w=[RpP[bk]])
                    pp = pP[bk]
                    KM = 255
                    if ck == 'rope' and not (KM & 2):
                        continue
                    if ck == 'v' and not (KM & 4):
                        continue
                    if ck == 'c2' and not (KM & 8):
                        continue
                    KR = 255
                    if ck == 'rope':
                      xr = ktok[s2].rearrange('p (g e) -> p g e', e=64)[:, :, 0:16]
                      cbc = mk(tab[:, 0:1], [(0, 8), (1, 16)])
                      sbc = mk(tab[:, 16:17], [(0, 8), (1, 16)])
                      kv3 = ktok[s2].rearrange('p (g e) -> p g e', e=64)
                      if KR & 1:
                        K.op('act', lambda e, pp=pp: e.activation(out=ktok[s2], in_=pp[:, 0:512], func=AF.Copy),
                             r=[RpP[bk]], w=[Rktok[s2]])
                      if KR & 2:
                        K.op('dve', lambda e, xr=xr, cbc=cbc: e.tensor_tensor(out=ta[s2], in0=xr, in1=cbc,
                                                                               op=ALU.mult),
                             r=[Rktok[s2], Rtab[i % NTB]], w=[Rta[s2]])
                        K.op('dve', lambda e, xr=xr, sbc=sbc: e.tensor_tensor(out=tb[s2], in0=xr, in1=sbc,
                                                                               op=ALU.mult),
                             r=[Rktok[s2], Rtab[i % NTB]], w=[Rtb[s2]])
                      if KR & 4:
                        E1 = None
                        K.op('pool', lambda e: e.tensor_tensor(out=ro[s2], in0=ta[s2], in1=tb[s2], op=ALU.subtract) if E1 else e.tensor_tensor(out=ro[s2][:, :, 0:8], in0=ta[s2][:, :, 0:8],
                                                              in1=tb[s2][:, :, 8:16], op=ALU.subtract),
                             r=[Rta[s2], Rtb[s2]], w=[Rro[s2]])
                        K.op('pool', lambda e: e.tensor_tensor(out=ro[s2], in0=ta[s2], in1=tb[s2], op=ALU.add) if E1 else e.tensor_tensor(out=ro[s2][:, :, 8:16], in0=ta[s2][:, :, 8:16],
                                                              in1=tb[s2][:, :, 0:8], op=ALU.add),
                             r=[Rta[s2], Rtb[s2]], w=[Rro[s2]])
                        K.op('act', lambda e, kv3=kv3: e.activation(out=kv3[:, :, 0:16], in_=ro[s2], func=AF.Copy),
                             r=[Rro[s2]], w=[Rktok[s2]])
                    elif ck == 'v':
                        K.op('dve', lambda e, pp=pp: e.tensor_copy(
                            out=Vst[ss][:, sub, 0:4, 0:128], in_=pp[:, 0:512].rearrange('p (h e) -> p h e', e=128)),
                            r=[RpP[bk]], w=[RVstA[ss][sub]])
                    elif ck in ('ga', 'gb'):
                        h0 = 0 if ck == 'ga' else 4
                        K.op('act', lambda e, pp=pp, h0=h0: e.activation(
                            out=Vst[ss][:, sub, h0:h0 + 4, 0:128], in_=pp[:, 0:512].rearrange('p (h e) -> p h e', e=128),
                            func=AF.Silu), r=[RpP[bk]], w=[(RVstA if ck == 'ga' else RVstB)[ss][sub]])
                    else:
                        K.op('act', lambda e, pp=pp: e.activation(out=junk[:, 0:nc2], in_=pp[:, 0:nc2],
                                                                  func=AF.Square, accum_out=st[:, 3:4]),
                             r=[RpP[bk]], w=[Rst[s4][3], Rjunk])
                        K.op('act', lambda e: e.activation(out=st[:, 4:5], in_=st[:, 3:4], func=AF.Sqrt,
                                                           scale=1.0 / nc2, bias=EPS),
                             r=[Rst[s4][3]], w=[Rst[s4][4]])
                        K.op('dve', lambda e: e.reciprocal(out=st[:, 5:6], in_=st[:, 4:5]),
                             r=[Rst[s4][4]], w=[Rst[s4][5]])
                        ncp = 320 if kv else nc2
                        K.op('act', lambda e, pp=pp: e.activation(out=c2b[s2][:, 0:ncp], in_=pp[:, 0:ncp], func=AF.Copy),
                             r=[RpP[bk]], w=[Rc2b[s2]])
                        if kv:
                            xk = c2b[s2][:, 256:320]
                            cb = tab[:, 32:96]
                            sn = tab[:, 96:160]
                            qa = tq_a[s2][:, 0, :]
                            qb = tq_b[s2][:, 0, :]
                            K.op('dve', lambda e, xk=xk, cb=cb, qa=qa: e.tensor_tensor(out=qa, in0=xk, in1=cb,
                                                                                       op=ALU.mult),
                                 r=[Rc2b[s2], Rtab[i % NTB]], w=[Rtqa[s2]])
                            K.op('dve', lambda e, xk=xk, sn=sn, qb=qb: e.tensor_tensor(out=qb, in0=xk, in1=sn,
                                                                                       op=ALU.mult),
                                 r=[Rc2b[s2], Rtab[i % NTB]], w=[Rtqb[s2]])
                            K.op('dve', lambda e, qa=qa, qb=qb: e.tensor_tensor(
                                out=krtok[s2][:, 0:32], in0=qa[:, 0:32], in1=qb[:, 32:64], op=ALU.subtract),
                                r=[Rtqa[s2], Rtqb[s2]], w=[Rkrtok[s2]])
                            K.op('dve', lambda e, qa=qa, qb=qb: e.tensor_tensor(
                                out=krtok[s2][:, 32:64], in0=qa[:, 32:64], in1=qb[:, 0:32], op=ALU.add),
                                r=[Rtqa[s2], Rtqb[s2]], w=[Rkrtok[s2]])

            def Y(i):
                s, s2, ss, sub, s4 = slots(i)
                for h in range(4):
                    K.op('pe', lambda e, h=h: e.transpose(psK[:, h * 128:(h + 1) * 128],
                                                          ktok[s2][:, h * 128:(h + 1) * 128], ident),
                         r=[Rktok[s2], Rident], w=[RpsK])
                K.op('dve', lambda e: e.tensor_copy(
                    out=KTst[ss][:, :, sub * 128:(sub + 1) * 128],
                    in_=psK[:, 0:512].rearrange('p (h t) -> p h t', t=128)), r=[RpsK], w=[RKTst[ss][sub]])
                for k in range(nc2 // 128):
                    K.op('pe', lambda e, k=k: e.transpose(psC[:, k * 128:(k + 1) * 128],
                                                          c2b[s2][:, k * 128:(k + 1) * 128], ident),
                         r=[Rc2b[s2], Rident], w=[RpsC])
                K.op('act', lambda e: e.activation(out=c2T[s2][:, 0:nc2], in_=psC[:, 0:nc2], func=AF.Copy),
                     r=[RpsC], w=[Rc2T[s2]])
                if kv:
                    K.op('pe', lambda e: e.transpose(psC64[:, 384:512], krtok[s2][:, 0:64], ident),
                         r=[Rkrtok[s2], Rident], w=[RpsC])
                    K.op('act', lambda e: e.activation(out=RTst[ss][0:64, 0, sub * 128:(sub + 1) * 128],
                                                       in_=psC64[:, 384:512], func=AF.Copy), r=[RpsC], w=[RRTst[ss][sub]])

            def Z(i):
                s, s2, ss, sub, s4 = slots(i)
                st = stt[s4]
                for (o0, on) in ((0, 512), (512, nout - 512)):
                    for k in range(nk):
                        K.op('pe', lambda e, o0=o0, on=on, k=k: e.matmul(
                            pKV[:, o0:o0 + on], lhsT=c2T[s2][:, k * 128:(k + 1) * 128], rhs=w2[:, k, o0:o0 + on],
                            start=(k == 0), stop=(k == nk - 1)), r=[Rc2T[s2], Rw2], w=[RpKV])
                if kv:
                    v4 = pKV.rearrange('p (h e) -> p h e', e=256)
                    K.op('dve', lambda e: e.tensor_scalar(out=n2tok[s2], in0=v4[:, :, 0:128], scalar1=st[:, 5:6],
                                                          scalar2=None, op0=ALU.mult),
                         r=[RpKV, Rst[s4][5]], w=[Rn2tok[s2]])
                    K.op('dve', lambda e: e.tensor_scalar(out=Vst[ss][:, sub, 4:8, 0:128], in0=v4[:, :, 128:256],
                                                          scalar1=st[:, 5:6], scalar2=None, op0=ALU.mult),
                         r=[RpKV, Rst[s4][5]], w=[RVstB[ss][sub]])
                else:
                    v4 = pKV[:, 0:768].rearrange('p (h e) -> p h e', e=192)
                    K.op('dve', lambda e: e.tensor_scalar(out=n2tok[s2], in0=v4[:, :, 0:128], scalar1=st[:, 5:6],
                                                          scalar2=None, op0=ALU.mult),
                         r=[RpKV, Rst[s4][5]], w=[Rn2tok[s2]])
                    xq = mk(pKV[:, 128:129], [(192, 4), (1, 64)])
                    tab = tabs[i % NTB]
                    cb = mk(tab[:, 32:33], [(0, 4), (1, 64)])
                    sn = mk(tab[:, 96:97], [(0, 4), (1, 64)])
                    K.op('dve', lambda e: e.scalar_tensor_tensor(out=tq_a[s2], in0=xq, scalar=st[:, 5:6], in1=cb,
                                                                 op0=ALU.mult, op1=ALU.mult),
                         r=[RpKV, Rst[s4][5], Rtab[i % NTB]], w=[Rtqa[s2]])
                    K.op('dve', lambda e: e.scalar_tensor_tensor(out=tq_b[s2], in0=xq, scalar=st[:, 5:6], in1=sn,
                                                                 op0=ALU.mult, op1=ALU.mult),
                         r=[RpKV, Rst[s4][5], Rtab[i % NTB]], w=[Rtqb[s2]])
                    K.op('dve', lambda e: e.tensor_tensor(out=qrtok[s2][:, :, 0:32], in0=tq_a[s2][:, :, 0:32],
                                                          in1=tq_b[s2][:, :, 32:64], op=ALU.subtract),
                         r=[Rtqa[s2], Rtqb[s2]], w=[Rqrtok[s2]])
                    K.op('dve', lambda e: e.tensor_tensor(out=qrtok[s2][:, :, 32:64], in0=tq_a[s2][:, :, 32:64],
                                                          in1=tq_b[s2][:, :, 0:32], op=ALU.add),
                         r=[Rtqa[s2], Rtqb[s2]], w=[Rqrtok[s2]])

            def W(i):
                s, s2, ss, sub, s4 = slots(i)
                for h in range(4):
                    K.op('pe', lambda e, h=h: e.transpose(psK[:, 512 + h * 128:512 + (h + 1) * 128],
                                                          n2tok[s2][:, h, :], ident),
                         r=[Rn2tok[s2], Rident], w=[RpsK])
                K.op('dve', lambda e: e.tensor_copy(
                    out=N2Tst[ss][:, :, sub * 128:(sub + 1) * 128],
                    in_=psK[:, 512:1024].rearrange('p (h t) -> p h t', t=128)),
                    r=[RpsK], w=[RN2Tst[ss][sub]])
                if not kv:
                    for h in range(4):
                        K.op('pe', lambda e, h=h: e.transpose(psC64[:, 512 + h * 128:512 + (h + 1) * 128],
                                                              qrtok[s2][:, h, :], ident),
                             r=[Rqrtok[s2], Rident], w=[RpsC])
                    K.op('act', lambda e: e.activation(
                        out=RTst[ss][0:64, :, sub * 128:(sub + 1) * 128],
                        in_=psC64[:, 512:1024].rearrange('p (h t) -> p h t', t=128), func=AF.Copy),
                        r=[RpsC], w=[RRTst[ss][sub]])
                if sub == NSUB - 1:
                    t0 = (i - (NSUB - 1)) * 128
                    j0 = i - (NSUB - 1)
                    nt = NSUB * 128
                    if kv:
                        d_kt, d_n2 = jb['KT'][0:4, :, t0:t0 + nt], jb['KT'][4:8, :, t0:t0 + nt]
                        d_r = jb['KrT'][:, t0:t0 + nt]
                        s_r = RTst[ss][0:64, 0, :]
                        d_v = jb['V'][:, :, j0:j0 + NSUB, :]
                        s_v = Vst[ss]
                    else:
                        d_kt, d_n2 = jb['QT'][0:4, :, t0:t0 + nt], jb['QT'][4:8, :, t0:t0 + nt]
                        d_r = jb['QrT'][:, :, t0:t0 + nt].rearrange('h d t -> d h t')
                        s_r = RTst[ss][0:64, :, :]
                        d_v = jb['G'][:, :, j0:j0 + NSUB, :]
                        s_v = Vst[ss][:, :, :, 0:128]
                    K.op('pool', lambda e: e.dma_start(out=d_kt.rearrange('h d t -> d h t'), in_=KTst[ss]),
                         r=RKTst[ss], dsem=st_sem[0][ss])
                    K.op('pool', lambda e: e.dma_start(out=d_n2.rearrange('h d t -> d h t'), in_=N2Tst[ss]),
                         r=RN2Tst[ss], dsem=st_sem[1][ss])
                    K.op('pool', lambda e: e.dma_start(out=d_r, in_=s_r), r=RRTst[ss], dsem=st_sem[2][ss])
                    for sb_ in range(NSUB):
                        K.op('pool', lambda e, sb_=sb_: e.dma_start(
                            out=d_v[:, :, sb_, :].rearrange('h p e -> p h e'), in_=s_v[:, sb_, :, :]),
                            r=[RVstA[ss][sb_], RVstB[ss][sb_]], dsem=st_sem[3 + sb_][ss])

            for i0 in range(min(3, NT)):
                ld(i0)
            pre(0)
            if NT > 1:
                pre(1)
            T1(0)
            for i in range(NT + 4):
                if i + 2 < NT:
                    pre(i + 2)
                if i + 1 < NT:
                    T1(i + 1)
                if 0 <= i - 4 < NT:
                    W(i - 4)
                if i < NT:
                    M(i)
                if 0 <= i - 1 < NT:
                    Y(i - 1)
                if 0 <= i - 2 < NT:
                    Z(i - 2)
                if i + 3 < NT:
                    ld(i + 3)
            sup_ctr[0] += NT // NSUB

        if KSTOP >= 2:
            for j in range(2):
                proj_pass(j, 'kv')
                if KSTOP >= 3:
                    proj_pass(j, 'q')
            K.barrier()

        def attn(j):
            jb = jobs[j]
            S, NQ = jb['S'], jb['NQ']
            NKB = S // 128
            HK = NKB // 2
            A.off = 0
            identb = A.alloc([128], BF16)
            KTb = [A.alloc([S], BF16) for _ in range(2)]
            Vb = [A.alloc([NKB, 129], BF16) for _ in range(2)]
            KrTp = A.alloc([S // 2], BF16)
            QA = [A.alloc([1024], BF16) for _ in range(2)]
            QB = [A.alloc([3, 1024], BF16) for _ in range(2)]
            Gb = [A.alloc([8, 128], BF16) for _ in range(2)]
            pT = [A.alloc([1024], BF16) for _ in range(3)]
            accsb = [A.alloc([9, 129], F32) for _ in range(2)]
            rr = [A.alloc([8], F32) for _ in range(2)]
            r2 = [A.alloc([4], F32) for _ in range(2)]
            ss4 = [A.alloc([4], F32) for _ in range(2)]
            lnv = [A.alloc([4], F32) for _ in range(2)]
            rs4 = [A.alloc([4], F32) for _ in range(2)]
            oo = [A.alloc([8, 128], F32) for _ in range(2)]
            tt = A.alloc([4, 128], F32)
            sq = A.alloc([4, 128], F32)
            yb = [A.alloc([8, 128], BF16) for _ in range(2)]
            gsub = A.alloc([128], F32)
            gsubf = A.alloc([128], F32)
            lamt = A.alloc([256], F32)
            lprod = A.alloc([128], F32)
            lsum = A.alloc([2], F32)
            lexp = A.alloc([2], F32)
            ldif = A.alloc([1], F32)
            neglam = A.alloc([1], F32)
            RKT = [Res(), Res()]
            RV = [Res(), Res()]
            RKr = Res()
            RKr2 = Res()
            RQ = [[Res(), Res(), Res()] for _ in range(2)]
            RG = [Res(), Res()]
            RpT = [Res(), Res(), Res()]
            Rsc = [Res(), Res()]
            Racc = Res()
            Raccsb = [Res(), Res()]
            Rrr = [Res(), Res()]
            Rr2 = [Res(), Res()]
            Rss4 = [Res(), Res()]
            Rlnv = [Res(), Res()]
            Rrs4 = [Res(), Res()]
            Roo = [Res(), Res()]
            Rtt, Rsq = Res(), Res()
            Ryb = [Res(), Res()]
            Rc = Res()
            Rgsub, Rlamt, Rlprod, Rlsum, Rlexp, Rldif = (Res() for _ in range(6))
            kt_sem = [K.newsem('kt%d_%d' % (j, i)) for i in range(2)]
            v_sem = [K.newsem('v%d_%d' % (j, i)) for i in range(2)]
            kr_sem = [K.newsem('kr%d_%d' % (j, i)) for i in range(2)]
            q_sem = [[K.newsem('q%d_%d_%d' % (j, i, k)) for k in range(3)] for i in range(2)]
            g_sem = [K.newsem('g%d_%d' % (j, i)) for i in range(2)]
            y_sem = [K.newsem('y%d_%d' % (j, i)) for i in range(2)]
            c_sem = [K.newsem('c%d_%d' % (j, i)) for i in range(2)]

            K.op('sp', lambda e: e.dma_start(out=gsub, in_=gsub_d), w=[Rgsub], dsem=c_sem[0])
            K.op('sp', lambda e: e.dma_start(out=lamt, in_=lam_d), w=[Rlamt], dsem=c_sem[1])
            K.op('dve', lambda e: e.tensor_scalar(out=gsubf, in0=gsub, scalar1=1.0 - LAMBDA_INIT, scalar2=None,
                                                  op0=ALU.mult), r=[Rgsub, Rlamt], w=[Rc])
            K.op('dve', lambda e: e.tensor_tensor(out=lprod, in0=lamt[:, 0:128], in1=lamt[:, 128:256], op=ALU.mult),
                 r=[Rlamt, Rgsub], w=[Rlprod])
            K.op('dve', lambda e: e.tensor_reduce(out=lsum, in_=lprod.rearrange('p (a b) -> p a b', b=64),
                                                  axis=AX.X, op=ALU.add), r=[Rlprod], w=[Rlsum])
            K.op('act', lambda e: e.activation(out=lexp, in_=lsum, func=AF.Exp), r=[Rlsum], w=[Rlexp])
            K.op('dve', lambda e: e.tensor_tensor(out=ldif, in0=lexp[:, 1:2], in1=lexp[:, 0:1], op=ALU.subtract),
                 r=[Rlexp], w=[Rldif])
            K.op('dve', lambda e: e.tensor_scalar(out=neglam, in0=ldif, scalar1=-LAMBDA_INIT, scalar2=None,
                                                  op0=ALU.add), r=[Rldif], w=[Rc])
            for s in range(2):
                K.op('pool', lambda e, s=s: e.memset(QA[s][64:128, 0:512], 0.0), w=[RQ[s][0]])
                K.op('pool', lambda e, s=s: e.memset(QA[s][0:64, 512:1024], 0.0), w=[RQ[s][1]])
                K.op('pool', lambda e, s=s: e.memset(QB[s][64:128, 1, :], 0.0), w=[RQ[s][1]])
                K.op('pool', lambda e, s=s: e.memset(QB[s][0:64, 2, :], 0.0), w=[RQ[s][2]])

            segs = []
            for h in range(8):
                qn = 512 if h < 4 else 1024
                for qc in range(NQ // qn):
                    segs.append((h, qc, qn))

            def load_head(h):
                b = h % 2
                K.op('sp', lambda e: e.dma_start(out=KTb[b], in_=jb['KT'][h]), w=[RKT[b]], dsem=kt_sem[b])
                K.op('sp', lambda e: e.dma_start(out=Vb[b], in_=jb['V'][h]), w=[RV[b]], dsem=v_sem[b])

            def load_seg(g):
                h, qc, qn = segs[g]
                s = g % 2
                q0 = qc * qn
                if h < 4:
                    K.op('sp', lambda e: e.dma_start(out=QA[s][0:64, 0:512], in_=jb['QT'][h, 0:64, q0:q0 + 512]),
                         w=[RQ[s][0]], dsem=q_sem[s][0])
                    K.op('sp', lambda e: e.dma_start(out=QA[s][64:128, 512:1024],
                                                     in_=jb['QT'][h, 64:128, q0:q0 + 512]),
                         w=[RQ[s][1]], dsem=q_sem[s][1])
                    K.op('sp', lambda e: e.dma_start(out=Gb[s][:, 0:4, :],
                                                     in_=jb['G'][h, :, q0 // 128:q0 // 128 + 4, :]),
                         w=[RG[s]], dsem=g_sem[s])
                else:
                    K.op('sp', lambda e: e.dma_start(out=QB[s][:, 0, :], in_=jb['QT'][h, :, q0:q0 + 1024]),
                         w=[RQ[s][0]], dsem=q_sem[s][0])
                    K.op('sp', lambda e: e.dma_start(out=QB[s][0:64, 1, :], in_=jb['QrT'][h - 4, :, q0:q0 + 1024]),
                         w=[RQ[s][1]], dsem=q_sem[s][1])
                    K.op('sp', lambda e: e.dma_start(out=QB[s][64:128, 2, :],
                                                     in_=jb['QrT'][h - 4, :, q0:q0 + 1024]),
                         w=[RQ[s][2]], dsem=q_sem[s][2])
                    K.op('sp', lambda e: e.dma_start(out=Gb[s][:, 0:8, :],
                                                     in_=jb['G'][h, :, q0 // 128:q0 // 128 + 8, :]),
                         w=[RG[s]], dsem=g_sem[s])


            tiles = []
            for g, (h, qc, qn) in enumerate(segs):
                for kb in range(NKB):
                    tiles.append((g, kb))
            T = len(tiles)
            sc = [PS[:, 0:1024], PS[:, 1024:2048]]

            def accap(idx):
                bank, pos = idx // 3, idx % 3
                return psf(4 + bank, 129, pos * 129)

            def QK(t):
                g, kb = tiles[t]
                h, qc, qn = segs[g]
                b, s = h % 2, g % 2
                so = sc[t % 2]
                lk = KTb[b][:, kb * 128:(kb + 1) * 128]
                if h < 4:
                    for hf in range(2):
                        K.op('pe', lambda e, hf=hf: e.matmul(so[:, hf * 512:(hf + 1) * 512], lhsT=lk,
                                                             rhs=QA[s][:, hf * 512:(hf + 1) * 512],
                                                             start=True, stop=True),
                             r=[RKT[b]] + RQ[s], w=[Rsc[t % 2]])
                else:
                    kr = KrTp[:, (kb % HK) * 128:(kb % HK + 1) * 128]
                    qi = 1 if kb < HK else 2
                    for hf in range(2):
                        K.op('pe', lambda e, hf=hf: e.matmul(so[:, hf * 512:(hf + 1) * 512], lhsT=lk,
                                                             rhs=QB[s][:, 0, hf * 512:(hf + 1) * 512],
                                                             start=True, stop=False),
                             r=[RKT[b]] + RQ[s], w=[Rsc[t % 2]])
                        K.op('pe', lambda e, hf=hf: e.matmul(so[:, hf * 512:(hf + 1) * 512], lhsT=kr,
                                                             rhs=QB[s][:, qi, hf * 512:(hf + 1) * 512],
                                                             start=False, stop=True),
                             r=[RKr, RKr2] + RQ[s], w=[Rsc[t % 2]])

            def EX(t):
                g, kb = tiles[t]
                h = segs[g][0]
                scale = 64 ** -0.5 if h < 4 else 192 ** -0.5
                K.op('act', lambda e: e.activation(out=pT[t % 3], in_=sc[t % 2], func=AF.Exp, scale=scale),
                     r=[Rsc[t % 2]], w=[RpT[t % 3]])

            def PV(t):
                g, kb = tiles[t]
                h = segs[g][0]
                b = h % 2
                for a in range(8):
                    K.op('pe', lambda e, a=a: e.matmul(accap(a), lhsT=pT[t % 3][:, a * 128:(a + 1) * 128],
                                                       rhs=Vb[b][:, kb, :], start=(kb == 0 and a % 3 == 0),
                                                       stop=(kb == NKB - 1), skip_group_check=True),
                         r=[RpT[t % 3], RV[b]], w=[Racc])

            deferred = []

            def EP1(g):
                h, qc, qn = segs[g]
                s, p = g % 2, g % 2
                A9 = accsb[p]
                for bk in range(3):
                    n = 387 if bk < 2 else 258
                    K.op('dve', lambda e, bk=bk, n=n: e.tensor_copy(
                        out=A9.rearrange('p a b -> p (a b)')[:, bk * 387:bk * 387 + n], in_=psf(4 + bk, n)),
                        r=[Racc], w=[Raccsb[p]])
                K.op('dve', lambda e: e.reciprocal(out=rr[p], in_=A9[:, 0:8, 128]), r=[Raccsb[p]], w=[Rrr[p]])
                if h < 4:
                    K.op('dve', lambda e: e.tensor_scalar(out=r2[p], in0=rr[p][:, 4:8], scalar1=neglam[:, 0:1],
                                                          scalar2=None, op0=ALU.mult), r=[Rrr[p], Rc], w=[Rr2[p]])
                    K.op('dve', lambda e: e.tensor_tensor(out=tt, in0=A9[:, 4:8, 0:128],
                                                          in1=mk(r2[p][:, 0:1], [(1, 4), (0, 128)]), op=ALU.mult),
                         r=[Raccsb[p], Rr2[p]], w=[Rtt])
                    o4 = oo[p][:, 0:4, :]
                    K.op('dve', lambda e: e.tensor_tensor(out=o4, in0=A9[:, 0:4, 0:128],
                                                          in1=mk(rr[p][:, 0:1], [(1, 4), (0, 128)]), op=ALU.mult),
                         r=[Raccsb[p], Rrr[p]], w=[Roo[p]])
                    K.op('dve', lambda e: e.tensor_tensor(out=o4, in0=o4, in1=tt, op=ALU.add),
                         r=[Roo[p], Rtt], w=[Roo[p]])
                    K.op('pool', lambda e: e.tensor_tensor(out=sq, in0=o4, in1=o4, op=ALU.mult),
                         r=[Roo[p]], w=[Rsq])
                    K.op('dve', lambda e: e.tensor_reduce(out=ss4[p], in_=sq, axis=AX.X, op=ALU.add),
                         r=[Rsq], w=[Rss4[p]])
                else:
                    K.op('dve', lambda e: e.tensor_tensor(out=oo[p], in0=A9[:, 0:8, 0:128],
                                                          in1=mk(rr[p][:, 0:1], [(1, 8), (0, 128)]), op=ALU.mult),
                         r=[Raccsb[p], Rrr[p]], w=[Roo[p]])

            def EP2(g):
                h, qc, qn = segs[g]
                s, p = g % 2, g % 2
                q0 = qc * qn
                nsb = qn // 128
                if h < 4:
                    o4 = oo[p][:, 0:4, :]
                    K.op('act', lambda e: e.activation(out=lnv[p], in_=ss4[p], func=AF.Ln, scale=1.0 / 128,
                                                       bias=EPS), r=[Rss4[p]], w=[Rlnv[p]])
                    K.op('act', lambda e: e.activation(out=rs4[p], in_=lnv[p], func=AF.Exp, scale=-0.5),
                         r=[Rlnv[p]], w=[Rrs4[p]])
                    K.op('dve', lambda e: e.tensor_tensor(out=o4, in0=o4,
                                                          in1=mk(rs4[p][:, 0:1], [(1, 4), (0, 128)]), op=ALU.mult),
                         r=[Roo[p], Rrs4[p]], w=[Roo[p]])
                    K.op('pool', lambda e: e.tensor_tensor(out=o4, in0=o4,
                                                           in1=mk(gsubf[:, 0:1], [(0, 4), (1, 128)]), op=ALU.mult),
                         r=[Roo[p], Rc], w=[Roo[p]])
                    K.op('dve', lambda e: e.tensor_tensor(out=yb[p][:, 0:4, :], in0=o4, in1=Gb[s][:, 0:4, :],
                                                          op=ALU.mult), r=[Roo[p], RG[s]], w=[Ryb[p]])
                else:
                    K.op('pool', lambda e: e.tensor_tensor(out=yb[p], in0=oo[p], in1=Gb[s], op=ALU.mult),
                         r=[Roo[p], RG[s]], w=[Ryb[p]])
                ybase = jb['Y'][q0:q0 + 128, h * 128:(h + 1) * 128]
                dst = mk(ybase, [(128 * D, nsb), (1, 128)])
                K.op('pool', lambda e: e.dma_start(out=dst, in_=yb[p][:, 0:nsb, :]), r=[Ryb[p]], dsem=y_sem[p])

            load_head(0)
            load_seg(0)
            if len(segs) > 1:
                load_seg(1)
            load_head(1)
            kr_loaded = [False]

            def load_kr():
                K.op('sp', lambda e: e.dma_start(out=KrTp[0:64, :], in_=jb['KrT'][:, 0:S // 2]), w=[RKr],
                     dsem=kr_sem[0])
                K.op('sp', lambda e: e.dma_start(out=KrTp[64:128, :], in_=jb['KrT'][:, S // 2:S]), w=[RKr2],
                     dsem=kr_sem[1])
            load_kr()
            QK(0)
            if T > 1:
                QK(1)
            for t in range(T):
                g, kb = tiles[t]
                h = segs[g][0]
                EX(t)
                if t + 2 < T:
                    QK(t + 2)
                PV(t)
                for d in list(deferred):
                    d[0] -= 1
                    if d[0] <= 0:
                        deferred.remove(d)
                        d[1]()
                if kb == NKB - 1:
                    EP1(g)
                    deferred.append([4, lambda g=g: EP2(g)])
                    if g + 2 < len(segs):
                        deferred.append([6, lambda g=g: load_seg(g + 2)])
                    if (g + 1 == len(segs) or segs[g + 1][0] != h) and h + 2 < 8:
                        load_head(h + 2)
            for d in deferred:
                d[1]()
            K.barrier()

        if KSTOP >= 4:
            attn(0)
        if KSTOP >= 5:
            attn(1)

        A.off = 0
        identc = A.alloc([128], BF16)
        w_out_sb = A.alloc([8, D], BF16)
        ggp = [A.alloc([D], F32) for _ in range(2)]
        wst2 = [A.alloc([D], F32) for _ in range(2)]
        yt = [A.alloc([D], BF16) for _ in range(2)]
        xt = [A.alloc([D], F32) for _ in range(2)]
        yT = [A.alloc([8, 128], BF16) for _ in range(2)]
        t1 = [A.alloc([D], F32) for _ in range(2)]
        res = [A.alloc([D], F32) for _ in range(2)]
        junk2 = A.alloc([D], BF16)
        so = [A.alloc([8], F32) for _ in range(4)]
        Ridc, Rwout = Res(), Res()
        Rjunk2 = [Res(), Res()]
        Rggp = [Res(), Res()]
        Rwst2 = [Res(), Res()]
        Ryt = [Res(), Res()]
        Rxt = [Res(), Res()]
        RyT = [Res(), Res()]
        Rt1 = [Res(), Res()]
        Rres = [Res(), Res()]
        Rso = [[Res() for _ in range(5)] for _ in range(4)]
        RpsT2 = Res()
        RpO = [Res(), Res()]
        o_ws = [K.newsem('ows0'), K.newsem('ows1')]
        o_y = [K.newsem('oy0'), K.newsem('oy1')]
        o_x = [K.newsem('ox0'), K.newsem('ox1')]
        o_st = [K.newsem('ost0'), K.newsem('ost1')]
        o_c = K.newsem('oc')
        K.op('sp', lambda e: e.dma_start(out=identc, in_=ident_d), w=[Ridc], dsem=o_c)
        for j in range(2):
            K.op('sp', lambda e, j=j: e.dma_start(out=ggp[j], in_=jobs[j]['GGP']), w=[Rggp[j]],
                 dsem=K.newsem('oggp%d' % j))
        for k in range(8):
            s = k % 2
            K.op('sp', lambda e, s=s, k=k: e.dma_start(out=wst2[s], in_=wout_d[k * 128:(k + 1) * 128, :]),
                 w=[Rwst2[s]], dsem=o_ws[s])
            K.op('dve' if k % 2 == 0 else 'pool', lambda e, s=s, k=k: e.tensor_copy(out=w_out_sb[:, k, :],
                                                                                    in_=wst2[s]),
                 r=[Rwst2[s]], w=[Rwout])
        psT2 = psb(0)
        pO = [PS[:, 1024:2048], PS[:, 2048:3072]]
        it = [0]
        out_final = []
        for j in range(2 if KSTOP >= 6 else 0):
            jb = jobs[j]
            for i in range(jb['NQ'] // 128):
                n = it[0]
                it[0] += 1
                s2, s4 = n % 2, n % 4
                rows = slice(i * 128, (i + 1) * 128)
                K.op('sp', lambda e, s2=s2, rows=rows, jb=jb: e.dma_start(out=yt[s2], in_=jb['Y'][rows, :]),
                     w=[Ryt[s2]], dsem=o_y[s2])
                K.op('sp', lambda e, s2=s2, rows=rows, jb=jb: e.dma_start(out=xt[s2], in_=jb['xq'][rows, :]),
                     w=[Rxt[s2]], dsem=o_x[s2])
                for c in range(8):
                    K.op('pe', lambda e, c=c, s2=s2: e.transpose(psT2[:, c * 128:(c + 1) * 128],
                                                                yt[s2][:, c * 128:(c + 1) * 128], identc),
                         r=[Ryt[s2], Ridc], w=[RpsT2])
                K.op('act', lambda e, s2=s2: e.activation(out=yT[s2].rearrange('p a b -> p (a b)'), in_=psT2,
                                                          func=AF.Copy), r=[RpsT2], w=[RyT[s2]])
                for hf in range(2):
                    for c in range(8):
                        K.op('pe', lambda e, c=c, hf=hf, s2=s2: e.matmul(
                            pO[s2][:, hf * 512:(hf + 1) * 512], lhsT=yT[s2][:, c, :],
                            rhs=w_out_sb[:, c, hf * 512:(hf + 1) * 512], start=(c == 0), stop=(c == 7)),
                            r=[RyT[s2], Rwout], w=[RpO[s2]])
                st = so[s4]
                for hf in range(2):
                    K.op('act', lambda e, hf=hf, s2=s2, st=st: e.activation(
                        out=junk2[:, hf * 512:(hf + 1) * 512], in_=pO[s2][:, hf * 512:(hf + 1) * 512], func=AF.Square,
                        accum_out=st[:, hf:hf + 1]), r=[RpO[s2]], w=[Rso[s4][hf], Rjunk2[hf]])
                K.op('dve', lambda e, st=st: e.tensor_tensor(out=st[:, 2:3], in0=st[:, 0:1], in1=st[:, 1:2],
                                                             op=ALU.add), r=[Rso[s4][0], Rso[s4][1]], w=[Rso[s4][2]])
                K.op('act', lambda e, st=st: e.activation(out=st[:, 3:4], in_=st[:, 2:3], func=AF.Sqrt,
                                                          scale=1.0 / D, bias=EPS), r=[Rso[s4][2]], w=[Rso[s4][3]])
                K.op('dve', lambda e, st=st: e.reciprocal(out=st[:, 4:5], in_=st[:, 3:4]), r=[Rso[s4][3]],
                     w=[Rso[s4][4]])
                for hf in range(2):
                    K.op('dve', lambda e, st=st, s2=s2, j=j, hf=hf: e.scalar_tensor_tensor(
                        out=t1[s2][:, hf * 512:(hf + 1) * 512], in0=pO[s2][:, hf * 512:(hf + 1) * 512],
                        scalar=st[:, 4:5], in1=ggp[j][:, hf * 512:(hf + 1) * 512], op0=ALU.mult, op1=ALU.mult),
                        r=[RpO[s2], Rso[s4][4], Rggp[j]], w=[Rt1[s2]])
                K.op('pool', lambda e, s2=s2: e.tensor_tensor(out=res[s2], in0=t1[s2], in1=xt[s2], op=ALU.add),
                     r=[Rt1[s2], Rxt[s2]], w=[Rres[s2]])
                K.op('pool', lambda e, s2=s2, rows=rows, jb=jb: e.dma_start(out=jb['y'][rows, :], in_=res[s2]),
                     r=[Rres[s2]], dsem=o_st[s2])

        with nc.Block() as block:
            K.finalize(block)
    return nc


def rope_table(pos):
    theta = np.float32(500000.0)
    out = np.zeros((len(pos), 160), np.float32)
    p = pos.astype(np.float32)[:, None]
    for dim, c0, s0, half in ((16, 0, 16, 8), (64, 32, 96, 32)):
        inv = theta ** (-np.arange(0, dim, 2, dtype=np.float32) / np.float32(dim))
        ang = (p * inv[None, :].astype(np.float32)).astype(np.float32)
        out[:, c0:c0 + half] = np.cos(ang).astype(np.float32)
        out[:, c0 + half:c0 + 2 * half] = out[:, c0:c0 + half]
        out[:, s0:s0 + half] = np.sin(ang).astype(np.float32)
        out[:, s0 + half:s0 + 2 * half] = out[:, s0:s0 + half]
    return out


_NC_CACHE = {}


def kernel(x_prompt, x_sample, c_prompt, c_sample, w_ada, b_ada, g_pre, w_in, lambda_q1, lambda_k1,
           lambda_q2, lambda_k2, g_subln, g_cq, w_uq, g_ckv, w_ukv, w_out, g_post):
    f = lambda a: np.ascontiguousarray(np.asarray(a, dtype=np.float32))
    x_prompt, x_sample = f(x_prompt), f(x_sample)
    B, SPr, _ = x_prompt.shape
    DB, SSm, _ = x_sample.shape
    ncore = 8
    assert B == ncore
    per = ncore // DB
    NQS = SSm // per
    key = (SPr, SSm, NQS)
    if key not in _NC_CACHE:
        _NC_CACHE[key] = build(SPr, SSm, NQS)
    nc = _NC_CACHE[key]
    col = lambda v: np.ascontiguousarray(f(v).reshape(-1, 128).T)
    bc = lambda v: np.ascontiguousarray(np.broadcast_to(f(v).reshape(1, -1), (128, f(v).size)))
    tabP = rope_table(np.arange(SPr))
    tabS = rope_table(np.arange(SSm))
    gcol = np.zeros((128, 8), np.float32)
    gcol[:, 0:3] = col(g_cq[0])
    gcol[:, 3:5] = col(g_ckv[0])
    lam = np.concatenate([f(lambda_q1[0]), f(lambda_q2[0]), f(lambda_k1[0]), f(lambda_k2[0])])
    shared = {
        'w_ada': f(w_ada[0]), 'bada_bc': bc(b_ada[0]), 'gpre_bc': bc(g_pre[0]), 'gpost_bc': bc(g_post[0]),
        'gsub_bc': bc(g_subln[0]), 'lam_bc': bc(lam), 'gcol': gcol, 'w_in': f(w_in[0]), 'w_uq': f(w_uq[0]),
        'w_ukv': f(w_ukv[0]), 'w_out': f(w_out[0]), 'ident': np.eye(128).astype(ml_dtypes.bfloat16),
        'tabP': tabP, 'tabS': tabS,
    }
    c_prompt, c_sample = f(c_prompt), f(c_sample)
    in_maps = []
    for c in range(ncore):
        sidx, r = c // per, c % per
        m = dict(shared)
        m['xkvP'] = x_prompt[c]
        m['xkvS'] = x_sample[sidx]
        m['xqS'] = np.ascontiguousarray(x_sample[sidx, r * NQS:(r + 1) * NQS])
        m['tabqS'] = np.ascontiguousarray(tabS[r * NQS:(r + 1) * NQS])
        m['ccol'] = np.ascontiguousarray(np.concatenate([col(c_prompt[c]), col(c_sample[sidx])], axis=1))
        in_maps.append(m)
    res = run_bass_kernel_spmd(nc, in_maps, core_ids=list(range(ncore)))
    yP = np.stack([np.asarray(res.results[c]['yP'], dtype=np.float32) for c in range(ncore)], axis=0)
    yS = np.zeros_like(x_sample)
    for c in range(ncore):
        sidx, r = c // per, c % per
        yS[sidx, r * NQS:(r + 1) * NQS] = np.asarray(res.results[c]['yS'], dtype=np.float32)
    return (yP, yS)
```
